# Optimizing a Trainium2 kernel written in Bass

```python
import math
import jax
import jax.numpy as jnp
from jax import lax
import numpy as np

D_MODEL = 1024
BATCH = 2
SEQ = 8192
DEPTH = 2

HEAD_DIM = 64
A_HEADS = 16
A_WIDTH = A_HEADS * HEAD_DIM
DILATED_PAIRS = ((128, 1), (512, 4), (2048, 16))
B_HEADS = D_MODEL // (2 * HEAD_DIM)
B_QK_WIDTH = 2 * B_HEADS * HEAD_DIM
B_V_DIM = 2 * HEAD_DIM
B_WIDTH = B_HEADS * B_V_DIM
IN_WIDTHS = (A_WIDTH, A_WIDTH, A_WIDTH, A_WIDTH,
             B_QK_WIDTH, B_QK_WIDTH, B_WIDTH, B_WIDTH,
             D_MODEL, D_MODEL)
IN_TOTAL = sum(IN_WIDTHS)
ROPE_THETA = 500000.0
ROPE_DIM = HEAD_DIM // 4
Q_BLOCK = 128
RMS_EPS = 1e-6
SUBLN_EPS = 1e-5
NEG = -1e30

kernel_name = 'hybrid_dilated_diff_gated_trunk'


def rms_norm(x, w, eps):
    xf = x.astype(jnp.float32)
    return xf * lax.rsqrt(jnp.mean(xf * xf, axis=-1, keepdims=True) + eps) * w.astype(jnp.float32)


def rope_tables(seq):
    inv = 1.0 / (ROPE_THETA ** (jnp.arange(0, ROPE_DIM, 2, dtype=jnp.float32) / ROPE_DIM))
    ang = jnp.arange(seq, dtype=jnp.float32)[:, None] * inv[None, :]
    return jnp.cos(ang), jnp.sin(ang)


def partial_rope(t, cos, sin):
    half = ROPE_DIM // 2
    tr = t[..., :ROPE_DIM].astype(jnp.float32)
    t1, t2 = tr[..., :half], tr[..., half:]
    c = cos[None, :, None, :]
    s = sin[None, :, None, :]
    rot = jnp.concatenate([t1 * c - t2 * s, t1 * s + t2 * c], axis=-1).astype(t.dtype)
    return jnp.concatenate([rot, t[..., ROPE_DIM:]], axis=-1)


def split_cols(u):
    outs = []
    start = 0
    for w in IN_WIDTHS:
        outs.append(u[..., start:start + w])
        start += w
    return outs


def dilated_pattern(q, k, v, window, dilation):
    b, s, h, dh = q.shape
    n = window // dilation
    chunk = dilation * n
    sp = -(-s // chunk) * chunk
    length = sp // dilation
    nb = length // n

    def to_blocks(t):
        t = jnp.pad(t, ((0, 0), (0, sp - s), (0, 0), (0, 0)))
        t = t.reshape(b, length, dilation, h, dh).transpose(0, 2, 1, 3, 4)
        return t.reshape(b, dilation, nb, n, h, dh)

    def with_prev(t):
        prev = jnp.pad(t, ((0, 0), (0, 0), (1, 0), (0, 0), (0, 0), (0, 0)))[:, :, :-1]
        return jnp.concatenate([prev, t], axis=3)

    qb = to_blocks(q)
    kb = with_prev(to_blocks(k))
    vb = with_prev(to_blocks(v))
    scores = jnp.einsum('brnqhd,brnkhd->brnqhk', qb, kb,
                        preferred_element_type=jnp.float32) * (HEAD_DIM ** -0.5)
    qi = jnp.arange(n)[:, None]
    kc = jnp.arange(2 * n)[None, :]
    dist = qi + n - kc
    blk = jnp.arange(nb)[:, None, None]
    valid = (dist >= 0) & (dist <= n) & (blk * n + kc - n >= 0)
    scores = jnp.where(valid[None, None, :, :, None, :], scores, NEG)
    m = jnp.max(scores, axis=-1)
    p = jnp.exp(scores - m[..., None])
    den = jnp.sum(p, axis=-1)
    o = jnp.einsum('brnqhk,brnkhd->brnqhd', p, vb.astype(jnp.float32)) / den[..., None]

    def from_blocks(t):
        t = t.reshape((b, dilation, length) + t.shape[4:])
        t = jnp.swapaxes(t, 1, 2)
        return t.reshape((b, sp) + t.shape[3:])[:, :s]

    return from_blocks(o), from_blocks(m), from_blocks(den)


def dilated_mixture(q, k, v):
    outs = [dilated_pattern(q, k, v, w, d) for (w, d) in DILATED_PAIRS]
    m_all = jnp.stack([r[1] for r in outs])
    d_all = jnp.stack([r[2] for r in outs])
    o_all = jnp.stack([r[0] for r in outs])
    wts = d_all * jnp.exp(m_all - jnp.max(m_all, axis=0))
    return jnp.sum(wts[..., None] * o_all, axis=0) / jnp.sum(wts, axis=0)[..., None]


def diff_attention(q1, q2, k1, k2, v, lam):
    b, s, h, dh = q1.shape
    nq = s // Q_BLOCK
    qs = jnp.stack([q1, q2], 0).reshape(2, b, nq, Q_BLOCK, h, dh).transpose(2, 0, 1, 3, 4, 5)
    ks = jnp.stack([k1, k2], 0)
    vf = v.astype(jnp.float32)
    kpos = jnp.arange(s)

    def one_block(args):
        qblk, bi = args
        sc = jnp.einsum('mbqhd,mbkhd->mbhqk', qblk, ks,
                        preferred_element_type=jnp.float32) * (dh ** -0.5)
        qpos = bi * Q_BLOCK + jnp.arange(Q_BLOCK)
        sc = jnp.where(kpos[None, :] <= qpos[:, None], sc, NEG)
        p = jax.nn.softmax(sc, axis=-1)
        a = p[0] - lam * p[1]
        return jnp.einsum('bhqk,bkhe->bqhe', a, vf)

    out = lax.map(one_block, (qs, jnp.arange(nq)))
    return out.transpose(1, 0, 2, 3, 4).reshape(b, s, h, 2 * dh)


def lambda_init_value(layer):
    return 0.8 - 0.6 * math.exp(-0.3 * layer)


def hybrid_layer(x, layer, norm_w, w_in, lq1, lk1, lq2, lk2, subln_w,
                 w_proj_a, w_proj_b, w_out, cos, sin):
    b, s, _ = x.shape
    hdn = rms_norm(x, norm_w, RMS_EPS).astype(x.dtype)
    u = hdn @ w_in
    qa, ka, va, za, qb, kb, vb, zb, ga, gb = split_cols(u)

    qa = partial_rope(qa.reshape(b, s, A_HEADS, HEAD_DIM), cos, sin)
    ka = partial_rope(ka.reshape(b, s, A_HEADS, HEAD_DIM), cos, sin)
    va = va.reshape(b, s, A_HEADS, HEAD_DIM)
    ya = dilated_mixture(qa, ka, va).reshape(b, s, A_WIDTH)
    ya = ya * jax.nn.silu(za.astype(jnp.float32))

    qb = partial_rope(qb.reshape(b, s, 2 * B_HEADS, HEAD_DIM), cos, sin)
    kb = partial_rope(kb.reshape(b, s, 2 * B_HEADS, HEAD_DIM), cos, sin)
    vb = vb.reshape(b, s, B_HEADS, B_V_DIM)
    lam_init = lambda_init_value(layer)
    lam = (jnp.exp(jnp.sum(lq1.astype(jnp.float32) * lk1.astype(jnp.float32)))
           - jnp.exp(jnp.sum(lq2.astype(jnp.float32) * lk2.astype(jnp.float32))) + lam_init)
    ob = diff_attention(qb[:, :, 0::2], qb[:, :, 1::2], kb[:, :, 0::2], kb[:, :, 1::2], vb, lam)
    ob = rms_norm(ob, subln_w, SUBLN_EPS) * (1.0 - lam_init)
    yb = ob.reshape(b, s, B_WIDTH) * jax.nn.silu(zb.astype(jnp.float32))

    pa = ya.astype(x.dtype) @ w_proj_a
    pb = yb.astype(x.dtype) @ w_proj_b
    merged = (jax.nn.sigmoid(ga.astype(jnp.float32)) * pa.astype(jnp.float32)
              + jax.nn.sigmoid(gb.astype(jnp.float32)) * pb.astype(jnp.float32))
    return x + (merged.astype(x.dtype) @ w_out).astype(x.dtype)


def setup_inputs(seed: int = 0) -> dict:
    key = jax.random.key(seed)
    ks = jax.random.split(key, 16)
    f32 = jnp.float32
    x = jax.random.normal(ks[0], (BATCH, SEQ, D_MODEL), f32)
    norm_w = 1.0 + 0.02 * jax.random.normal(ks[1], (DEPTH, D_MODEL), f32)
    w_in = jax.random.normal(ks[2], (DEPTH, D_MODEL, IN_TOTAL), f32) * D_MODEL ** -0.5
    lambda_q1 = 0.1 * jax.random.normal(ks[3], (DEPTH, HEAD_DIM), f32)
    lambda_k1 = 0.1 * jax.random.normal(ks[4], (DEPTH, HEAD_DIM), f32)
    lambda_q2 = 0.1 * jax.random.normal(ks[5], (DEPTH, HEAD_DIM), f32)
    lambda_k2 = 0.1 * jax.random.normal(ks[6], (DEPTH, HEAD_DIM), f32)
    subln_w = 1.0 + 0.02 * jax.random.normal(ks[7], (DEPTH, B_V_DIM), f32)
    w_proj_a = jax.random.normal(ks[8], (DEPTH, A_WIDTH, D_MODEL), f32) * A_WIDTH ** -0.5
    w_proj_b = jax.random.normal(ks[9], (DEPTH, B_WIDTH, D_MODEL), f32) * B_WIDTH ** -0.5
    w_out = jax.random.normal(ks[10], (DEPTH, D_MODEL, D_MODEL), f32) * D_MODEL ** -0.5
    final_norm_w = 1.0 + 0.02 * jax.random.normal(ks[11], (D_MODEL,), f32)
    return {'x': x, 'norm_w': norm_w, 'w_in': w_in,
            'lambda_q1': lambda_q1, 'lambda_k1': lambda_k1,
            'lambda_q2': lambda_q2, 'lambda_k2': lambda_k2,
            'subln_w': subln_w, 'w_proj_a': w_proj_a, 'w_proj_b': w_proj_b,
            'w_out': w_out, 'final_norm_w': final_norm_w}


def reference(x, norm_w, w_in, lambda_q1, lambda_k1, lambda_q2, lambda_k2,
              subln_w, w_proj_a, w_proj_b, w_out, final_norm_w):
    cos, sin = rope_tables(x.shape[1])
    h = x
    for layer in range(DEPTH):
        h = hybrid_layer(h, layer, norm_w[layer], w_in[layer],
                         lambda_q1[layer], lambda_k1[layer], lambda_q2[layer], lambda_k2[layer],
                         subln_w[layer], w_proj_a[layer], w_proj_b[layer], w_out[layer],
                         cos, sin)
    return rms_norm(h, final_norm_w, RMS_EPS).astype(x.dtype)
```

```python
import math
import contextlib
import numpy as np
import ml_dtypes
import concourse.bass as bass
import concourse.mybir as mybir
from concourse.bass_utils import run_bass_kernel_spmd

F32 = mybir.dt.float32
BF16 = mybir.dt.bfloat16
AF = mybir.ActivationFunctionType
ALU = mybir.AluOpType

SEQ = 8192
DM = 1024
TOWN = 2048
DEPTH = 2
NEG = -30000.0
RMS_EPS = 1e-6
SUBLN_EPS = 1e-5
OFF = dict(qA=0, kA=1024, vA=2048, zA=3072, qB=4096, kB=5120, vB=6144, zB=7168, ga=8192, gb=9216)


def lambda_init_value(layer):
    return 0.8 - 0.6 * math.exp(-0.3 * layer)


class Prog:
    def __init__(self, nc):
        self.nc = nc
        self.ops = []

    def add(self, eng, fn, reads=(), writes=(), sem=None):
        self.ops.append(dict(eng=eng, fn=fn, reads=tuple(reads), writes=tuple(writes),
                             chan=(sem if sem is not None else eng), dma=sem is not None))

    def emit(self):
        nc = self.nc
        ops = self.ops
        last_w = {}
        readers = {}
        deps = [None] * len(ops)
        needed = [False] * len(ops)
        for i, op in enumerate(ops):
            d = set()
            for r in op['reads']:
                if r in last_w:
                    d.add(last_w[r])
            for w in op['writes']:
                if w in last_w:
                    d.add(last_w[w])
                for j in readers.get(w, ()):
                    d.add(j)
            d.discard(i)
            for r in op['reads']:
                readers.setdefault(r, []).append(i)
            for w in op['writes']:
                last_w[w] = i
                readers[w] = []
            dd = set()
            for j in d:
                pj = ops[j]
                if (not pj['dma']) and pj['eng'] == op['eng'] and op['eng'] == 'pe' and not op['dma']:
                    continue
                dd.add(j)
            deps[i] = dd
            for j in dd:
                needed[j] = True
        chans = []
        for op in ops:
            if op['chan'] not in chans:
                chans.append(op['chan'])
        count = {c: 0 for c in chans}
        value = [0] * len(ops)
        last_dma = {}
        for i, op in enumerate(ops):
            if op['dma']:
                needed[i] = True
                last_dma[op['chan']] = i
            if needed[i]:
                count[op['chan']] += 16 if op['dma'] else 1
                value[i] = count[op['chan']]
        sems = {}
        with contextlib.ExitStack() as st:
            for n, c in enumerate(chans):
                sems[c] = st.enter_context(nc.semaphore("s%d" % n))
            block = st.enter_context(nc.Block())
            deco = dict(pe=block.tensor, act=block.scalar, dve=block.vector,
                        pool=block.gpsimd, sp=block.sync)
            for e in ['pe', 'act', 'dve', 'pool', 'sp']:
                my = [i for i, op in enumerate(ops) if op['eng'] == e]
                if not my and e != 'sp':
                    continue

                def body(engine, my=my, e=e):
                    waited = {}
                    for i in my:
                        op = ops[i]
                        need = {}
                        for j in deps[i]:
                            c = ops[j]['chan']
                            need[c] = max(need.get(c, 0), value[j])
                        for c, v in need.items():
                            if waited.get(c, 0) >= v:
                                continue
                            engine.wait_ge(sems[c], v)
                            waited[c] = v
                        ins = op['fn'](engine)
                        if needed[i]:
                            ins.then_inc(sems[op['chan']], 16 if op['dma'] else 1)
                    if e == 'sp':
                        for c, i in last_dma.items():
                            if waited.get(c, 0) < value[i]:
                                engine.wait_ge(sems[c], value[i])
                deco[e](body)


class B:
    def __init__(self, nc):
        self.nc = nc
        self.p = Prog(nc)
        self.st = contextlib.ExitStack()
        self.nload = 0

    def sb(self, name, shape, dt):
        return self.st.enter_context(self.nc.sbuf_tensor(name, shape, dt))

    def ps(self, name, shape, dt):
        return self.st.enter_context(self.nc.psum_tensor(name, shape, dt))

    def mm(self, out, lhsT, rhs, start, stop, r, w):
        self.p.add('pe', lambda e: e.matmul(out=out, lhsT=lhsT, rhs=rhs, start=start, stop=stop), r, w)

    def tr(self, out, in_, ident, r, w):
        self.p.add('pe', lambda e: e.transpose(out=out, in_=in_, identity=ident), r, w)

    def act(self, out, in_, func, r, w, scale=None, bias=None, accum_out=None):
        kw = {}
        if scale is not None:
            kw['scale'] = scale
        if bias is not None:
            kw['bias'] = bias
        if accum_out is not None:
            kw['accum_out'] = accum_out
        self.p.add('act', lambda e: e.activation(out=out, in_=in_, func=func, **kw), r, w)

    def tt(self, eng, out, in0, in1, op, r, w):
        self.p.add(eng, lambda e: e.tensor_tensor(out=out, in0=in0, in1=in1, op=op), r, w)

    def ts(self, eng, out, in0, s1, op0, r, w, s2=None, op1=None, accum_out=None):
        kw = {}
        if op1 is not None:
            kw['op1'] = op1
        if accum_out is not None:
            kw['accum_out'] = accum_out
        self.p.add(eng, lambda e: e.tensor_scalar(out=out, in0=in0, scalar1=s1, scalar2=s2, op0=op0, **kw), r, w)

    def stt(self, out, in0, scalar, in1, op0, op1, r, w):
        self.p.add('dve', lambda e: e.scalar_tensor_tensor(out=out, in0=in0, scalar=scalar, in1=in1,
                                                            op0=op0, op1=op1), r, w)

    def cp(self, eng, out, in_, r, w):
        if eng == 'act':
            self.p.add('act', lambda e: e.activation(out=out, in_=in_, func=AF.Copy), r, w)
        else:
            self.p.add(eng, lambda e: e.tensor_copy(out=out, in_=in_), r, w)

    def recip(self, out, in_, r, w):
        self.p.add('dve', lambda e: e.reciprocal(out=out, in_=in_), r, w)

    def memset(self, eng, ap, val, w):
        self.p.add(eng, lambda e: e.memset(ap, val), (), w)

    def dma(self, out, in_, r, w, sem, eng='sp'):
        self.p.add(eng, lambda e: e.dma_start(out=out, in_=in_), r, w, sem=sem)

    def finish(self):
        self.p.emit()
        self.st.close()


def bcast_rows(ap2d, nparts):
    return bass.AP(tensor=ap2d.tensor, offset=ap2d.offset, ap=[[0, nparts]] + [list(x) for x in ap2d.ap[1:]])


class NormStage:
    def __init__(self, b, nw_dram, idb, tag):
        self.b = b
        self.tag = tag
        self.nw = b.sb("nw" + tag, [128, DM], F32)
        self.eps = b.sb("eps" + tag, [128, 1], F32)
        self.junk = b.sb("junk" + tag, [128, DM], BF16)
        self.st = [b.sb("nst%s%d" % (tag, i), [128, 4], F32) for i in range(2)]
        self.idb = idb
        b.dma(self.nw[:], bcast_rows(nw_dram, 128), [], ['nw' + tag], sem='ld_nw' + tag)
        b.memset('pool', self.eps[:], RMS_EPS, ['eps' + tag])

    def rstd(self, xt, xkey, i):
        b = self.b
        st = self.st[i % 2]
        k = ('nst' + self.tag, i % 2)
        b.act(self.junk[:], xt, AF.Square, [xkey], ['junk' + self.tag, k], accum_out=st[:, 0:1])
        b.act(st[:, 1:2], st[:, 0:1], AF.Ln, [k, 'eps' + self.tag], [k], scale=1.0 / DM, bias=self.eps[:, 0:1])
        b.act(st[:, 2:3], st[:, 1:2], AF.Exp, [k], [k], scale=-0.5)
        return st[:, 2:3], k


def build_norm_launch():
    nc = bass.Bass("TRN2", target_bir_lowering=False)
    x = nc.dram_tensor("x", [TOWN, DM], F32, kind="ExternalInput").ap()
    nw = nc.dram_tensor("nw", [1, DM], F32, kind="ExternalInput").ap()
    idn = nc.dram_tensor("idn", [128, 128], BF16, kind="ExternalInput").ap()
    hT = nc.dram_tensor("hT", [DM, TOWN], BF16, kind="ExternalOutput").ap()
    b = B(nc)
    idb = b.sb("idb", [128, 128], BF16)
    b.dma(idb[:], idn, [], ['idb'], sem='ld_id')
    ns = NormStage(b, nw, idb, "a")
    xt = [b.sb("xt%d" % i, [128, DM], F32) for i in range(2)]
    hb = [b.sb("hb%d" % i, [128, DM], BF16) for i in range(2)]
    hTt = [b.sb("hTt%d" % i, [128, 8, 512], BF16) for i in range(2)]
    PT = [b.ps("PT%d" % i, [128, 1024], BF16) for i in range(2)]
    hTv = hT.rearrange("(k p) t -> p k t", p=128)
    for t in range(TOWN // 128):
        s = t % 2
        b.dma(xt[s][:], x[t * 128:(t + 1) * 128, :], [], [('xt', s)], sem='ld_x%d' % s)
        r, rk = ns.rstd(xt[s][:], ('xt', s), t)
        b.stt(hb[s][:], xt[s][:], r, ns.nw[:], ALU.mult, ALU.mult, [('xt', s), rk, 'nwa'], [('hb', s)])
        for k in range(8):
            b.tr(PT[s][:, k * 128:(k + 1) * 128], hb[s][:, k * 128:(k + 1) * 128], idb[:], [('hb', s), 'idb'], [('PT', s)])
        g = t // 4
        gs = g % 2
        b.cp('dve' if t % 2 else 'act', hTt[gs][:, :, (t % 4) * 128:(t % 4 + 1) * 128],
             PT[s][:].rearrange("p (k n) -> p k n", k=8), [('PT', s)], [('hTt', gs, t % 4)])
        if t % 4 == 3:
            b.dma(hTv[:, :, g * 512:(g + 1) * 512], hTt[gs][:], [('hTt', gs, q) for q in range(4)], [], sem='st_h%d' % gs)
    b.finish()
    return nc


DBG = dict(ntt=16, parts='zvqrt')


def build_mixer_launch(layer, phases=(0, 1, 2, 3), stage=9):
    nc = bass.Bass("TRN2", target_bir_lowering=False)
    hT = nc.dram_tensor("hT", [DM, SEQ], BF16, kind="ExternalInput").ap()
    wm = nc.dram_tensor("wm", [4, DM, 512], F32, kind="ExternalInput").ap()
    lamv = nc.dram_tensor("lamv", [1, 256], F32, kind="ExternalInput").ap()
    subw = nc.dram_tensor("subw", [128, 1], F32, kind="ExternalInput").ap()
    idn = nc.dram_tensor("idn", [128, 128], BF16, kind="ExternalInput").ap()
    mka = nc.dram_tensor("mka", [128, 256], BF16, kind="ExternalInput").ap()
    mkb = nc.dram_tensor("mkb", [128, 128], BF16, kind="ExternalInput").ap()
    cosd = nc.dram_tensor("cosd", [128, 512], F32, kind="ExternalInput").ap()
    sind = nc.dram_tensor("sind", [128, 512], F32, kind="ExternalInput").ap()
    yT = nc.dram_tensor("yT", [512, SEQ], BF16, kind="ExternalOutput").ap()
    lam_init = lambda_init_value(layer)
    b = B(nc)
    idb = b.sb("idb", [128, 128], BF16)
    maskA = b.sb("maskA", [128, 256], BF16)
    maskB = b.sb("maskB", [128, 128], BF16)
    cosT = b.sb("cosT", [128, 64, 8], F32)
    sinT = b.sb("sinT", [128, 64, 8], F32)
    ones = b.sb("ones", [128, 128], BF16)
    onesS = b.sb("onesS", [128, 128], BF16)
    B1 = b.sb("B1", [64, 128], F32)
    B2 = b.sb("B2", [64, 128], F32)
    epsb = b.sb("epsb", [128, 1], F32)
    lt = b.sb("lt", [128, 256], F32)
    lj = b.sb("lj", [128, 128], F32)
    ls = b.sb("ls", [128, 8], F32)
    sw = b.sb("sw", [128, 2], F32)
    b.dma(idb[:], idn, [], ['idb'], sem='ld_c0')
    b.dma(maskA[:], mka, [], ['maskA'], sem='ld_c1')
    b.dma(maskB[:], mkb, [], ['maskB'], sem='ld_c2')
    b.dma(cosT[:].rearrange("p t i -> p (t i)"), cosd, [], ['cosT'], sem='ld_c3')
    b.dma(sinT[:].rearrange("p t i -> p (t i)"), sind, [], ['sinT'], sem='ld_c4')
    b.dma(lt[:], bcast_rows(lamv, 128), [], ['lt'], sem='ld_c5')
    b.dma(sw[:, 0:1], subw, [], ['sw0'], sem='ld_c6')
    b.memset('pool', ones[:], 1.0, ['ones'])
    b.memset('pool', onesS[:], 1.0 / 128, ['onesS'])
    b.memset('pool', B1[:], 0.0, ['B1'])
    b.memset('pool', B2[:], 0.0, ['B2'])
    b.memset('pool', B1[0:32, :], 1.0 / 32, ['B1'])
    b.memset('pool', B2[32:64, :], 1.0 / 32, ['B2'])
    b.memset('pool', epsb[:], SUBLN_EPS, ['epsb'])
    b.tt('dve', lj[:, 0:64], lt[:, 0:64], lt[:, 64:128], ALU.mult, ['lt'], ['lj'])
    b.tt('dve', lj[:, 64:128], lt[:, 128:192], lt[:, 192:256], ALU.mult, ['lt'], ['lj'])
    b.ts('dve', lt[:, 0:64], lj[:, 0:64], 1.0, ALU.mult, ['lj'], ['lt', 'ls0'], op1=ALU.add, accum_out=ls[:, 0:1])
    b.ts('dve', lt[:, 64:128], lj[:, 64:128], 1.0, ALU.mult, ['lj'], ['lt', 'ls1'], op1=ALU.add, accum_out=ls[:, 1:2])
    b.act(ls[:, 2:4], ls[:, 0:2], AF.Exp, ['ls0', 'ls1'], ['ls2'])
    b.tt('dve', ls[:, 4:5], ls[:, 3:4], ls[:, 2:3], ALU.subtract, ['ls2'], ['ls4'])
    b.ts('dve', ls[:, 5:6], ls[:, 4:5], -lam_init, ALU.add, ['ls4'], ['neglam'])
    b.ts('dve', sw[:, 1:2], sw[:, 0:1], 1.0 - lam_init, ALU.mult, ['sw0'], ['sw1'])
    neglam = ls[:, 5:6]
    wst = [b.sb("wst%d" % i, [128, 512], F32) for i in range(2)]
    wph = b.sb("wph", [128, 8, 512], BF16)
    hTt = [b.sb("hTt%d" % i, [128, 8, 512], BF16) for i in range(2)]
    qT = b.sb("qT", [128, SEQ], BF16)
    kT = b.sb("kT", [128, SEQ], BF16)
    zs = b.sb("zs", [128, SEQ], BF16)
    vT = b.sb("vT", [128, SEQ], BF16)
    acc = b.sb("acc", [128, SEQ], F32)
    V = b.sb("V", [128, 64, 128], BF16)
    Pb = [b.sb("Pb%d" % i, [128, 2, 512], BF16) for i in range(3)]
    stg = [b.sb("stg%d" % i, [128, 256], BF16) for i in range(4)]
    rt = [b.sb("rt%d" % i, [128, 4, 4, 8], F32) for i in range(2)]
    o1s = b.sb("o1s", [128, 512], F32)
    o2s = b.sb("o2s", [128, 512], F32)
    dens = b.sb("dens", [64, 512], F32)
    r1 = b.sb("r1", [128, 512], F32)
    r2 = b.sb("r2", [128, 512], F32)
    ob = b.sb("ob", [128, 512], F32)
    sq = b.sb("sq", [128, 512], BF16)
    rs = b.sb("rs", [128, 512], F32)
    yb = [b.sb("yb%d" % i, [128, 512], BF16) for i in range(2)]
    rr = b.sb("rr", [128, 1024], F32)
    ya = [b.sb("ya%d" % i, [128, 1024], BF16) for i in range(2)]
    PS = [b.ps("PS%d" % i, [128, 1024], F32) for i in range(4)]

    def bank(i):
        return PS[i // 2][:, (i % 2) * 512:(i % 2 + 1) * 512]

    def bank_bf(i):
        return bank(i).bitcast(BF16)

    hTv = hT.rearrange("(k p) t -> p k t", p=128)
    nload = [0]

    def load_h(tt):
        s = nload[0] % 2
        nload[0] += 1
        b.dma(hTt[s][:], hTv[:, :, tt * 512:(tt + 1) * 512], [], [('hTt', s)], sem='ld_h%d' % s)
        return s

    for ph in phases:
        isB = ph < 2
        for k in range(8):
            s = k % 2
            b.dma(wst[s][:], wm[ph, k * 128:(k + 1) * 128, :], [], [('wst', s)], sem='ld_w%d' % s)
            b.cp('pool', wph[:, k, :], wst[s][:], [('wst', s)], [('wph', k)])
        wkeys = [('wph', k) for k in range(8)]
        hs = load_h(0)
        pend = None
        for tt in range(DBG['ntt']):
            cur = hs
            if tt + 1 < DBG['ntt']:
                hs = load_h(tt + 1)
            hk = ('hTt', cur)
            for k in range(8 if 'z' in DBG['parts'] else 0):
                b.mm(bank(DBG.get('zb', 0)), wph[:, k, 384:512], hTt[cur][:, k, :], k == 0, k == 7, wkeys + [hk], [('bk', 0)])
            if 'z' in DBG['parts']:
                b.act(zs[:, tt * 512:(tt + 1) * 512], bank(DBG.get('zb', 0)), AF.Silu, [('bk', 0)], [('zs', tt)])
            for k in range(8 if 'v' in DBG['parts'] else 0):
                b.mm(bank(1), wph[:, k, 256:384], hTt[cur][:, k, :], k == 0, k == 7, wkeys + [hk], [('bk', 1)])
            if 'v' in DBG['parts']:
                b.cp('act', vT[:, tt * 512:(tt + 1) * 512], bank(1), [('bk', 1)], [('vT', tt)])
            pr = tt % 2
            for s4 in range(DBG.get('ns4', 4) if 'q' in DBG['parts'] else 0):
                qk = bank(2 + s4)[:, 0:256]
                qkk = ('bk', 2 + s4)
                for k in range(8):
                    b.mm(qk, hTt[cur][:, k, s4 * 128:(s4 + 1) * 128], wph[:, k, 0:256], k == 0, k == 7,
                         wkeys + [hk], [qkk])
                t = tt * 4 + s4
                qv = qk.rearrange("p (s d) -> p s d", s=4)
                sg = stg[s4]
                sgv = sg[:].rearrange("p (s d) -> p s d", s=4)
                sgk = ('stg', s4)
                R = rt[s4 % 2]
                rk = ('rt', s4 % 2)
                ca_ = cosT[:, t, :]
                cb = bass.AP(tensor=ca_.tensor, offset=ca_.offset, ap=[list(ca_.ap[0]), [0, 4], [1, 8]])
                sa_ = sinT[:, t, :]
                sb_ = bass.AP(tensor=sa_.tensor, offset=sa_.offset, ap=[list(sa_.ap[0]), [0, 4], [1, 8]])
                if 'r' not in DBG['parts']:
                    b.cp(DBG.get('qcp', 'act'), sg[:], qk, [qkk], [sgk])
                    continue
                t1 = qv[:, :, 0:8]
                t2 = qv[:, :, 8:16]
                b.tt('dve', R[:, 0, :, :], t1, cb, ALU.mult, [qkk, 'cosT'], [rk])
                b.tt('dve', R[:, 1, :, :], t2, sb_, ALU.mult, [qkk, 'sinT'], [rk])
                b.tt('dve', R[:, 2, :, :], t1, sb_, ALU.mult, [qkk, 'sinT'], [rk])
                b.tt('dve', R[:, 3, :, :], t2, cb, ALU.mult, [qkk, 'cosT'], [rk])
                b.tt('dve', sgv[:, :, 0:8], R[:, 0, :, :], R[:, 1, :, :], ALU.subtract, [rk], [sgk])
                b.tt('dve', sgv[:, :, 8:16], R[:, 2, :, :], R[:, 3, :, :], ALU.add, [rk], [sgk])
                b.cp('act', sgv[:, :, 16:64], qv[:, :, 16:64], [qkk], [sgk])
            tb = 6 + pr
            tbv = bank_bf(tb)
            if 't' not in DBG['parts']:
                continue
            for s4 in range(4):
                b.tr(tbv[:, s4 * 128:(s4 + 1) * 128], stg[s4][:, 0:128], idb[:], [('stg', s4), 'idb'], [('bk', tb)])
                b.tr(tbv[:, (4 + s4) * 128:(5 + s4) * 128], stg[s4][:, 128:256], idb[:], [('stg', s4), 'idb'], [('bk', tb)])
            b.cp('dve', qT[:, tt * 512:(tt + 1) * 512], tbv[:, 0:512], [('bk', tb)], [('qT', tt)])
            b.cp('dve', kT[:, tt * 512:(tt + 1) * 512], tbv[:, 512:1024], [('bk', tb)], [('kT', tt)])
        if stage < 2:
            continue
        allq = [('qT', i) for i in range(16)]
        allk = [('kT', i) for i in range(16)]
        allv = [('vT', i) for i in range(16)]
        allz = [('zs', i) for i in range(16)]
        if isB:
            for g8 in range(8):
                tb = 6 + g8 % 2
                tbv = bank_bf(tb)
                for j in range(8):
                    t = g8 * 8 + j
                    b.tr(tbv[:, j * 128:(j + 1) * 128], vT[:, t * 128:(t + 1) * 128], idb[:], allv + ['idb'], [('bk', tb)])
                b.cp('dve' if g8 % 2 else 'act', V[:, g8 * 8:(g8 + 1) * 8, :].rearrange("p a c -> p (a c)"), tbv[:, :],
                     [('bk', tb)], [('V', g8)])
            allV = [('V', i) for i in range(8)]
            if stage < 3:
                continue
            steps = []
            for qc in range(16):
                for kt in range(4 * qc + 4):
                    steps.append((qc, kt))
            LAG = 1

            def emit_S(n):
                qc, kt = steps[n]
                j = kt - 4 * qc
                c0 = 128 * j if j >= 0 else 0
                sp_ = n % 2
                for h2 in range(2):
                    Sb = bank(2 * sp_ + h2)
                    b.mm(Sb[:, c0:512], kT[64 * h2:64 * h2 + 64, kt * 128:(kt + 1) * 128],
                         qT[64 * h2:64 * h2 + 64, qc * 512 + c0:(qc + 1) * 512], True, j < 0,
                         allq + allk, [('bk', 2 * sp_ + h2)])
                if j >= 0:
                    for h2 in range(2):
                        Sb = bank(2 * sp_ + h2)
                        b.mm(Sb[:, c0:c0 + 128], idb[:], maskB[:], False, True, ['idb', 'maskB'], [('bk', 2 * sp_ + h2)])
                pb = Pb[n % 3]
                b.act(pb[:, :, c0:512], PS[sp_][:].rearrange("p (b n) -> p b n", b=2)[:, :, c0:512], AF.Exp,
                      [('bk', 2 * sp_), ('bk', 2 * sp_ + 1)], [('Pb', n % 3)], scale=0.125)

            def emit_AV(n):
                qc, kt = steps[n]
                j = kt - 4 * qc
                c0 = 128 * j if j >= 0 else 0
                first = kt == 0
                last = kt == 4 * qc + 3
                pb = Pb[n % 3]
                pk = [('Pb', n % 3)]
                b.mm(bank(4)[:, c0:512], V[:, kt, :], pb[:, 0, c0:512], first, last, allV + pk, [('bk', 4)])
                b.mm(bank(5)[:, c0:512], V[:, kt, :], pb[:, 1, c0:512], first, last, allV + pk, [('bk', 5)])
                b.mm(bank(6)[0:32, c0:512], ones[:, 0:32], pb[:, 0, c0:512], first, last, ['ones'] + pk, [('bk', 6)])
                b.mm(bank(6)[32:64, c0:512], ones[:, 0:32], pb[:, 1, c0:512], first, last, ['ones'] + pk, [('bk', 6)])
                if last:
                    epilogue(qc, n)

            pending = []

            def epilogue(qc, n):
                b.cp('dve', o1s[:], bank(4), [('bk', 4)], ['o1s'])
                b.cp('act', o2s[:], bank(5), [('bk', 5)], ['o2s'])
                b.cp('dve', dens[:], bank(6)[0:64, :], [('bk', 6)], ['dens'])

                def partB():
                    b.mm(bank(7), B1[:], dens[:], True, True, ['B1', 'dens'], [('bk', 7)])
                    b.recip(r1[:], bank(7), [('bk', 7)], ['r1'])
                    b.tt('dve', r1[:], o1s[:], r1[:], ALU.mult, ['o1s', 'r1'], ['r1'])

                def partC():
                    b.mm(bank(7), B2[:], dens[:], True, True, ['B2', 'dens'], [('bk', 7)])
                    b.recip(r2[:], bank(7), [('bk', 7)], ['r2'])
                    b.tt('dve', r2[:], o2s[:], r2[:], ALU.mult, ['o2s', 'r2'], ['r2'])
                    b.stt(ob[:], r2[:], neglam, r1[:], ALU.mult, ALU.add, ['r1', 'r2', 'neglam'], ['ob'])
                    b.tt('dve', sq[:], ob[:], ob[:], ALU.mult, ['ob'], ['sq'])

                def partD():
                    b.mm(bank(7), onesS[:], sq[:], True, True, ['onesS', 'sq'], [('bk', 7)])
                    b.act(rs[:], bank(7), AF.Ln, [('bk', 7), 'epsb'], ['rs'], bias=epsb[:, 0:1])
                    b.act(rs[:], rs[:], AF.Exp, ['rs'], ['rs'], scale=-0.5)
                    b.tt('dve', ob[:], ob[:], rs[:], ALU.mult, ['ob', 'rs'], ['ob'])
                    y = yb[qc % 2]
                    b.stt(y[:], ob[:], sw[:, 1:2], zs[:, qc * 512:(qc + 1) * 512], ALU.mult, ALU.mult,
                          ['ob', 'sw1'] + allz, [('yb', qc % 2)])
                    b.dma(yT[ph * 128:(ph + 1) * 128, qc * 512:(qc + 1) * 512], y[:], [('yb', qc % 2)], [],
                          sem='st_y%d' % (qc % 2))
                pending.append((n + 1, partB))
                pending.append((n + 2, partC))
                pending.append((n + 3, partD))

            for n in range(len(steps) + LAG):
                if n < len(steps):
                    emit_S(n)
                if n - LAG >= 0:
                    emit_AV(n - LAG)
                while pending and pending[0][0] <= n - LAG:
                    pending.pop(0)[1]()
            while pending:
                pending.pop(0)[1]()
        else:
            nblk = [0]
            for h in range(2):
                rb = 64 * h
                Vh = V[:, (32 * h):(32 * h + 32), :].rearrange("p a c -> p (a c)").rearrange("p (s d) -> p s d", d=64)
                b.memset('pool', acc[:], 0.0, [('acc', gi, r_) for gi in range(64) for r_ in range(16)])
                for D in (1, 4, 16):
                    NCH = 64 // D
                    for g16 in range(4):
                        tb = 6 + g16 % 2
                        tbv = bank_bf(tb)
                        for j in range(16):
                            slot = g16 * 16 + j
                            c, r_ = slot // D, slot % D
                            tok0 = c * 128 * D + r_
                            b.tr(tbv[:, j * 64:(j + 1) * 64], vT[rb:rb + 64, tok0:tok0 + 127 * D + 1:D], idb[rb:rb + 64, rb:rb + 64],
                                 allv + ['idb'], [('bk', tb)])
                        b.cp('act', Vh[:, g16 * 16:(g16 + 1) * 16, :].rearrange("p s d -> p (s d)"), tbv[:, :],
                             [('bk', tb)], [('Vh', h)] + [('V', g_) for g_ in range(8)])
                    corder = list(range(0, NCH, 2)) + list(range(1, NCH, 2))
                    for c in corder:
                        for r_ in range(D):
                            n = nblk[0]
                            nblk[0] += 1
                            tok0 = c * 128 * D + r_
                            nq = 256 if c + 1 < NCH else 128
                            Sb = bank(n % 4)
                            Ob = bank(4 + n % 2)
                            b.mm(Sb[:, 0:nq], kT[rb:rb + 64, tok0:tok0 + 127 * D + 1:D], qT[rb:rb + 64, tok0:tok0 + (nq - 1) * D + 1:D],
                                 True, False, allq + allk, [('bk', n % 4)])
                            b.mm(Sb[:, 0:nq], idb[:], maskA[:, 0:nq], False, True, ['idb', 'maskA'], [('bk', n % 4)])
                            pb = Pb[n % 3]
                            b.act(pb[:, 0, 0:nq], Sb[:, 0:nq], AF.Exp, [('bk', n % 4)], [('Pb', n % 3)], scale=0.125)
                            slot = c * D + r_
                            b.mm(Ob[rb:rb + 64, 0:nq], Vh[:, slot, :], pb[:, 0, 0:nq], True, True,
                                 [('Vh', h), ('Pb', n % 3)], [('bk', 4 + n % 2)])
                            b.mm(Ob[64 - rb:128 - rb, 0:nq], ones[:, 0:64], pb[:, 0, 0:nq], True, True,
                                 ['ones', ('Pb', n % 3)], [('bk', 4 + n % 2)])
                            g0 = tok0 // 128
                            ng = (nq * D) // 128
                            if D == 1:
                                keys = [('acc', g0 + gi, x_) for gi in range(ng) for x_ in range(16)]
                            elif D == 4:
                                keys = [('acc', g0 + gi, r_ + 4 * x_) for gi in range(ng) for x_ in range(4)]
                            else:
                                keys = [('acc', g0 + gi, r_) for gi in range(ng)]
                            av = acc[:, tok0:tok0 + (nq - 1) * D + 1:D]
                            b.tt('dve', av, Ob[:, 0:nq], av, ALU.add, [('bk', 4 + n % 2)] + keys, keys)
                allacc = [('acc', gi, x_) for gi in range(64) for x_ in range(16)]
                for pc in range(8):
                    cs = slice(pc * 1024, (pc + 1) * 1024)
                    y = ya[pc % 2]
                    b.recip(rr[rb:rb + 64, :], acc[64 - rb:128 - rb, cs], allacc, ['rr'])
                    b.tt('dve', rr[rb:rb + 64, :], acc[rb:rb + 64, cs], rr[rb:rb + 64, :], ALU.mult, allacc + ['rr'], ['rr'])
                    b.tt('dve', y[rb:rb + 64, :], rr[rb:rb + 64, :], zs[rb:rb + 64, cs], ALU.mult, ['rr'] + allz, [('ya', pc % 2)])
                    b.dma(yT[ph * 128 + rb:ph * 128 + rb + 64, cs], y[rb:rb + 64, :], [('ya', pc % 2)], [],
                          sem='st_a%d' % (pc % 2))
    b.finish()
    return nc


def build_merge_launch(last):
    nc = bass.Bass("TRN2", target_bir_lowering=False)
    yaT = nc.dram_tensor("yaT", [DM, TOWN], BF16, kind="ExternalInput").ap()
    ybT = nc.dram_tensor("ybT", [DM, TOWN], BF16, kind="ExternalInput").ap()
    hT = nc.dram_tensor("hT", [DM, TOWN], BF16, kind="ExternalInput").ap()
    x = nc.dram_tensor("x", [TOWN, DM], F32, kind="ExternalInput").ap()
    wp = nc.dram_tensor("wp", [5, DM, DM], F32, kind="ExternalInput").ap()
    nw = nc.dram_tensor("nw", [1, DM], F32, kind="ExternalInput").ap()
    idn = nc.dram_tensor("idn", [128, 128], BF16, kind="ExternalInput").ap()
    if last:
        out = nc.dram_tensor("out", [TOWN, DM], F32, kind="ExternalOutput").ap()
    else:
        xn_d = nc.dram_tensor("xn", [TOWN, DM], F32, kind="ExternalOutput").ap()
        hTn = nc.dram_tensor("hTn", [DM, TOWN], BF16, kind="ExternalOutput").ap()
    b = B(nc)
    idb = b.sb("idb", [128, 128], BF16)
    b.dma(idb[:], idn, [], ['idb'], sem='ld_id')
    ns = NormStage(b, nw, idb, "n")
    W = [b.sb("W%d" % i, [128, 8, DM], BF16) for i in range(5)]
    wst = [b.sb("wst%d" % i, [128, DM], F32) for i in range(2)]
    n = 0
    for wi in (2, 3, 0, 1, 4):
        for k in range(8):
            s = n % 2
            n += 1
            b.dma(wst[s][:], wp[wi, k * 128:(k + 1) * 128, :], [], [('wst', s)], sem='ld_w%d' % s)
            b.cp('pool', W[wi][:, k, :], wst[s][:], [('wst', s)], [('W', wi)])
    inT = [[b.sb("in%d_%d" % (i, s), [128, 8, 512], BF16) for s in range(1)] for i in range(3)]
    srcs = [yaT.rearrange("(k p) t -> p k t", p=128), ybT.rearrange("(k p) t -> p k t", p=128),
            hT.rearrange("(k p) t -> p k t", p=128)]
    sg = [b.sb("sg%d" % i, [128, 512], F32) for i in range(4)]
    m1 = [b.sb("m1_%d" % i, [128, 512], F32) for i in range(2)]
    m2 = [b.sb("m2_%d" % i, [128, 512], F32) for i in range(2)]
    mT = b.sb("mT", [128, 8, 512], BF16)
    xt = [b.sb("xt%d" % i, [128, DM], F32) for i in range(2)]
    xo = [b.sb("xo%d" % i, [128, DM], F32) for i in range(2)]
    hb = [b.sb("hb%d" % i, [128, DM], BF16) for i in range(2)]
    hTt = [b.sb("hTt%d" % i, [128, 8, 512], BF16) for i in range(2)]
    PS = [b.ps("PS%d" % i, [128, 512], F32) for i in range(7)]
    PT = b.ps("PT", [128, 1024], BF16)
    if not last:
        hTv = hTn.rearrange("(k p) t -> p k t", p=128)
    nx = 0
    for tq in range(TOWN // 512):
        s = 0
        for i in range(3):
            b.dma(inT[i][s][:], srcs[i][:, :, tq * 512:(tq + 1) * 512], [], [('in', i, s)], sem='ld_i%d_%d' % (i, s))
        for oc in range(8):
            pz = (oc % 2) * 2
            ocs = slice(oc * 128, (oc + 1) * 128)
            for k in range(8):
                b.mm(PS[pz][:], W[2][:, k, ocs], inT[2][s][:, k, :], k == 0, k == 7, [('W', 2), ('in', 2, s)], [('bk', pz)])
            for k in range(8):
                b.mm(PS[pz + 1][:], W[3][:, k, ocs], inT[2][s][:, k, :], k == 0, k == 7, [('W', 3), ('in', 2, s)], [('bk', pz + 1)])
            b.act(sg[pz][:], PS[pz][:], AF.Sigmoid, [('bk', pz)], [('sg', pz)])
            b.act(sg[pz + 1][:], PS[pz + 1][:], AF.Sigmoid, [('bk', pz + 1)], [('sg', pz + 1)])
            pp = 4 + (oc % 2)
            for k in range(8):
                b.mm(PS[pp][:], W[0][:, k, ocs], inT[0][s][:, k, :], k == 0, k == 7, [('W', 0), ('in', 0, s)], [('bk', pp)])
            b.tt('dve', m1[oc % 2][:], PS[pp][:], sg[pz][:], ALU.mult, [('bk', pp), ('sg', pz)], [('m1', oc % 2)])
            for k in range(8):
                b.mm(PS[6][:], W[1][:, k, ocs], inT[1][s][:, k, :], k == 0, k == 7, [('W', 1), ('in', 1, s)], [('bk', 6)])
            b.tt('dve', m2[oc % 2][:], PS[6][:], sg[pz + 1][:], ALU.mult, [('bk', 6), ('sg', pz + 1)], [('m2', oc % 2)])
            b.tt('pool', mT[:, oc, :], m1[oc % 2][:], m2[oc % 2][:], ALU.add, [('m1', oc % 2), ('m2', oc % 2)], [('mT', oc)])
        mkeys = [('mT', oc) for oc in range(8)]
        for s4 in range(4):
            t = tq * 4 + s4
            xs = nx % 2
            nx += 1
            b.dma(xt[xs][:], x[t * 128:(t + 1) * 128, :], [], [('xt', xs)], sem='ld_x%d' % xs)
            for half in range(2):
                pb_ = PS[half]
                for k in range(8):
                    b.mm(pb_[:], mT[:, k, s4 * 128:(s4 + 1) * 128], W[4][:, k, half * 512:(half + 1) * 512],
                         k == 0, k == 7, mkeys + [('W', 4)], [('bk', half)])
                b.tt('dve', xo[xs][:, half * 512:(half + 1) * 512], pb_[:], xt[xs][:, half * 512:(half + 1) * 512], ALU.add,
                     [('bk', half), ('xt', xs)], [('xo', xs, half)])
            xok = [('xo', xs, 0), ('xo', xs, 1)]
            if not last:
                b.dma(xn_d[t * 128:(t + 1) * 128, :], xo[xs][:], xok, [], sem='st_x%d' % xs)
            b.act(ns.junk[:], xo[xs][:], AF.Square, xok, ['junkn', ('nst', t % 2)], accum_out=ns.st[t % 2][:, 0:1])
            st_ = ns.st[t % 2]
            kk = ('nst', t % 2)
            b.act(st_[:, 1:2], st_[:, 0:1], AF.Ln, [kk, 'epsn'], [kk], scale=1.0 / DM, bias=ns.eps[:, 0:1])
            b.act(st_[:, 2:3], st_[:, 1:2], AF.Exp, [kk], [kk], scale=-0.5)
            if last:
                b.stt(xt[xs][:], xo[xs][:], st_[:, 2:3], ns.nw[:], ALU.mult, ALU.mult, xok + [kk, 'nwn', ('xt', xs)], [('xt', xs)])
                b.dma(out[t * 128:(t + 1) * 128, :], xt[xs][:], [('xt', xs)], [], sem='st_o%d' % xs)
            else:
                b.stt(hb[xs][:], xo[xs][:], st_[:, 2:3], ns.nw[:], ALU.mult, ALU.mult, xok + [kk, 'nwn'], [('hb', xs)])
                for k in range(8):
                    b.tr(PT[:, k * 128:(k + 1) * 128], hb[xs][:, k * 128:(k + 1) * 128], idb[:], [('hb', xs), 'idb'], ['PT'])
                gs = tq % 2
                b.cp('act', hTt[gs][:, :, s4 * 128:(s4 + 1) * 128], PT[:].rearrange("p (k n) -> p k n", k=8), ['PT'],
                     [('hTt', gs, s4)])
                if s4 == 3:
                    b.dma(hTv[:, :, tq * 512:(tq + 1) * 512], hTt[gs][:], [('hTt', gs, q) for q in range(4)], [],
                          sem='st_h%d' % gs)
    b.finish()
    return nc


def _consts():
    idn = np.eye(128, dtype=np.float32).astype(ml_dtypes.bfloat16)
    ki = np.arange(128)[:, None]
    qi = np.arange(128)[None, :]
    mka = np.concatenate([np.where(ki <= qi, 0.0, NEG), np.where(ki >= qi, 0.0, NEG)], axis=1).astype(np.float32)
    mkb = np.where(ki <= qi, 0.0, NEG).astype(np.float32)
    inv = (1.0 / (np.float32(500000.0) ** (np.arange(0, 16, 2, dtype=np.float32) / np.float32(16)))).astype(np.float32)
    pos = np.arange(SEQ, dtype=np.float32)
    ang = (pos[:, None] * inv[None, :]).astype(np.float32)
    cos = np.cos(ang).astype(np.float32).reshape(64, 128, 8).transpose(1, 0, 2).reshape(128, 512)
    sin = np.sin(ang).astype(np.float32).reshape(64, 128, 8).transpose(1, 0, 2).reshape(128, 512)
    return dict(idn=idn, mka=mka.astype(ml_dtypes.bfloat16), mkb=mkb.astype(ml_dtypes.bfloat16),
                cosd=np.ascontiguousarray(cos), sind=np.ascontiguousarray(sin))


def _mixer_weights(w_in_l, g):
    cols = []
    for j in range(2):
        hb = 2 * g + j
        for nm in ('qB', 'kB', 'vB', 'zB'):
            cols.append(np.arange(OFF[nm] + hb * 128, OFF[nm] + hb * 128 + 128))
    for j in range(2):
        ha = 4 * g + 2 * j
        for nm in ('qA', 'kA', 'vA', 'zA'):
            cols.append(np.arange(OFF[nm] + ha * 64, OFF[nm] + ha * 64 + 128))
    cols = np.concatenate(cols)
    w = w_in_l[:, cols]
    return np.ascontiguousarray(w.reshape(DM, 4, 512).transpose(1, 0, 2))


_NC_CACHE = {}


def _get(name, fn, *a):
    key = (name,) + a
    if key not in _NC_CACHE:
        _NC_CACHE[key] = fn(*a)
    return _NC_CACHE[key]


def kernel(x, norm_w, w_in, lambda_q1, lambda_k1, lambda_q2, lambda_k2, subln_w,
           w_proj_a, w_proj_b, w_out, final_norm_w):
    x = np.asarray(x, dtype=np.float32)
    cst = _consts()
    cores = list(range(8))
    xs = [np.ascontiguousarray(x[c // 4, (c % 4) * TOWN:(c % 4 + 1) * TOWN, :]) for c in cores]
    nca = build_norm_launch()
    res = run_bass_kernel_spmd(nca, [dict(x=xs[c], nw=np.ascontiguousarray(norm_w[0][None, :]), idn=cst['idn'])
                                     for c in cores], core_ids=cores)
    hown = [res.results[c]['hT'] for c in cores]
    for l in range(DEPTH):
        hfull = [np.ascontiguousarray(np.concatenate(hown[4 * bb:4 * bb + 4], axis=1)) for bb in range(2)]
        lamv = np.ascontiguousarray(np.concatenate([lambda_q1[l], lambda_k1[l], lambda_q2[l], lambda_k2[l]])[None, :].astype(np.float32))
        subw = np.ascontiguousarray(np.asarray(subln_w[l], dtype=np.float32).reshape(128, 1))
        ncm = build_mixer_launch(l)
        in_maps = []
        for c in cores:
            in_maps.append(dict(hT=hfull[c // 4], wm=_mixer_weights(np.asarray(w_in[l]), c % 4), lamv=lamv, subw=subw,
                                idn=cst['idn'], mka=cst['mka'], mkb=cst['mkb'], cosd=cst['cosd'], sind=cst['sind']))
        res = run_bass_kernel_spmd(ncm, in_maps, core_ids=cores)
        yTs = [res.results[c]['yT'] for c in cores]
        last = l == DEPTH - 1
        ncp = build_merge_launch(last)
        wp = np.ascontiguousarray(np.stack([np.asarray(w_proj_a[l]), np.asarray(w_proj_b[l]),
                                            np.asarray(w_in[l][:, OFF['ga']:OFF['ga'] + DM]),
                                            np.asarray(w_in[l][:, OFF['gb']:OFF['gb'] + DM]),
                                            np.asarray(w_out[l])]).astype(np.float32))
        nwn = np.ascontiguousarray((final_norm_w if last else norm_w[l + 1])[None, :].astype(np.float32))
        in_maps = []
        for c in cores:
            bb, g = c // 4, c % 4
            ts = slice(g * TOWN, (g + 1) * TOWN)
            yaT = np.ascontiguousarray(np.concatenate([yTs[4 * bb + gg][256:512, ts] for gg in range(4)], axis=0))
            ybT = np.ascontiguousarray(np.concatenate([yTs[4 * bb + gg][0:256, ts] for gg in range(4)], axis=0))
            in_maps.append(dict(yaT=yaT, ybT=ybT, hT=hown[c], x=xs[c], wp=wp, nw=nwn, idn=cst['idn']))
        res = run_bass_kernel_spmd(ncp, in_maps, core_ids=cores)
        if last:
            outs = [res.results[c]['out'] for c in cores]
        else:
            xs = [res.results[c]['xn'] for c in cores]
            hown = [res.results[c]['hTn'] for c in cores]
    out = np.empty((2, SEQ, DM), dtype=np.float32)
    for c in cores:
        out[c // 4, (c % 4) * TOWN:(c % 4 + 1) * TOWN, :] = outs[c]
    return out
```

```python
import math
import contextlib
import numpy as np
import ml_dtypes
import concourse.bass as bass
import concourse.mybir as mybir
from concourse.bass_utils import run_bass_kernel_spmd

F32 = mybir.dt.float32
BF16 = mybir.dt.bfloat16
AF = mybir.ActivationFunctionType
ALU = mybir.AluOpType

SEQ = 8192
DM = 1024
TOWN = 2048
DEPTH = 2
NEG = -30000.0
RMS_EPS = 1e-6
SUBLN_EPS = 1e-5
OFF = dict(qA=0, kA=1024, vA=2048, zA=3072, qB=4096, kB=5120, vB=6144, zB=7168, ga=8192, gb=9216)


def lambda_init_value(layer):
    return 0.8 - 0.6 * math.exp(-0.3 * layer)


class Prog:
    def __init__(self, nc):
        self.nc = nc
        self.ops = []

    def add(self, eng, fn, reads=(), writes=(), sem=None):
        self.ops.append(dict(eng=eng, fn=fn, reads=tuple(reads), writes=tuple(writes),
                             chan=(sem if sem is not None else eng), dma=sem is not None))

    def emit(self):
        nc = self.nc
        ops = self.ops
        last_w = {}
        readers = {}
        deps = [None] * len(ops)
        needed = [False] * len(ops)
        for i, op in enumerate(ops):
            d = set()
            for r in op['reads']:
                if r in last_w:
                    d.add(last_w[r])
            for w in op['writes']:
                if w in last_w:
                    d.add(last_w[w])
                for j in readers.get(w, ()):
                    d.add(j)
            d.discard(i)
            for r in op['reads']:
                readers.setdefault(r, []).append(i)
            for w in op['writes']:
                last_w[w] = i
                readers[w] = []
            dd = set()
            for j in d:
                pj = ops[j]
                if (not pj['dma']) and pj['eng'] == op['eng'] and op['eng'] == 'pe' and not op['dma']:
                    continue
                dd.add(j)
            deps[i] = dd
            for j in dd:
                needed[j] = True
        chans = []
        for op in ops:
            if op['chan'] not in chans:
                chans.append(op['chan'])
        count = {c: 0 for c in chans}
        value = [0] * len(ops)
        last_dma = {}
        for i, op in enumerate(ops):
            if op['dma']:
                needed[i] = True
                last_dma[op['chan']] = i
            if needed[i]:
                count[op['chan']] += 16 if op['dma'] else 1
                value[i] = count[op['chan']]
        sems = {}
        with contextlib.ExitStack() as st:
            for n, c in enumerate(chans):
                sems[c] = st.enter_context(nc.semaphore("s%d" % n))
            block = st.enter_context(nc.Block())
            deco = dict(pe=block.tensor, act=block.scalar, dve=block.vector,
                        pool=block.gpsimd, sp=block.sync)
            for e in ['pe', 'act', 'dve', 'pool', 'sp']:
                my = [i for i, op in enumerate(ops) if op['eng'] == e]
                if not my and e != 'sp':
                    continue

                def body(engine, my=my, e=e):
                    waited = {}
                    for i in my:
                        op = ops[i]
                        need = {}
                        for j in deps[i]:
                            c = ops[j]['chan']
                            need[c] = max(need.get(c, 0), value[j])
                        for c, v in need.items():
                            if waited.get(c, 0) >= v:
                                continue
                            engine.wait_ge(sems[c], v)
                            waited[c] = v
                        ins = op['fn'](engine)
                        if needed[i]:
                            ins.then_inc(sems[op['chan']], 16 if op['dma'] else 1)
                    if e == 'sp':
                        for c, i in last_dma.items():
                            if waited.get(c, 0) < value[i]:
                                engine.wait_ge(sems[c], value[i])
                deco[e](body)


class B:
    def __init__(self, nc):
        self.nc = nc
        self.p = Prog(nc)
        self.st = contextlib.ExitStack()
        self.nload = 0

    def sb(self, name, shape, dt):
        return self.st.enter_context(self.nc.sbuf_tensor(name, shape, dt))

    def ps(self, name, shape, dt):
        return self.st.enter_context(self.nc.psum_tensor(name, shape, dt))

    def mm(self, out, lhsT, rhs, start, stop, r, w):
        self.p.add('pe', lambda e: e.matmul(out=out, lhsT=lhsT, rhs=rhs, start=start, stop=stop), r, w)

    def tr(self, out, in_, ident, r, w):
        self.p.add('pe', lambda e: e.transpose(out=out, in_=in_, identity=ident), r, w)

    def act(self, out, in_, func, r, w, scale=None, bias=None, accum_out=None):
        kw = {}
        if scale is not None:
            kw['scale'] = scale
        if bias is not None:
            kw['bias'] = bias
        if accum_out is not None:
            kw['accum_out'] = accum_out
        self.p.add('act', lambda e: e.activation(out=out, in_=in_, func=func, **kw), r, w)

    def tt(self, eng, out, in0, in1, op, r, w):
        self.p.add(eng, lambda e: e.tensor_tensor(out=out, in0=in0, in1=in1, op=op), r, w)

    def ts(self, eng, out, in0, s1, op0, r, w, s2=None, op1=None, accum_out=None):
        kw = {}
        if op1 is not None:
            kw['op1'] = op1
        if accum_out is not None:
            kw['accum_out'] = accum_out
        self.p.add(eng, lambda e: e.tensor_scalar(out=out, in0=in0, scalar1=s1, scalar2=s2, op0=op0, **kw), r, w)

    def stt(self, out, in0, scalar, in1, op0, op1, r, w):
        self.p.add('dve', lambda e: e.scalar_tensor_tensor(out=out, in0=in0, scalar=scalar, in1=in1,
                                                            op0=op0, op1=op1), r, w)

    def cp(self, eng, out, in_, r, w):
        if eng == 'act':
            self.p.add('act', lambda e: e.activation(out=out, in_=in_, func=AF.Copy), r, w)
        else:
            self.p.add(eng, lambda e: e.tensor_copy(out=out, in_=in_), r, w)

    def recip(self, out, in_, r, w):
        self.p.add('dve', lambda e: e.reciprocal(out=out, in_=in_), r, w)

    def memset(self, eng, ap, val, w):
        self.p.add(eng, lambda e: e.memset(ap, val), (), w)

    def dma(self, out, in_, r, w, sem, eng='sp'):
        self.p.add(eng, lambda e: e.dma_start(out=out, in_=in_), r, w, sem=sem)

    def finish(self):
        self.p.emit()
        self.st.close()


def bcast_rows(ap2d, nparts):
    return bass.AP(tensor=ap2d.tensor, offset=ap2d.offset, ap=[[0, nparts]] + [list(x) for x in ap2d.ap[1:]])


class NormStage:
    def __init__(self, b, nw_dram, idb, tag):
        self.b = b
        self.tag = tag
        self.nw = b.sb("nw" + tag, [128, DM], F32)
        self.eps = b.sb("eps" + tag, [128, 1], F32)
        self.junk = b.sb("junk" + tag, [128, DM], BF16)
        self.st = [b.sb("nst%s%d" % (tag, i), [128, 4], F32) for i in range(2)]
        self.idb = idb
        b.dma(self.nw[:], bcast_rows(nw_dram, 128), [], ['nw' + tag], sem='ld_nw' + tag)
        b.memset('pool', self.eps[:], RMS_EPS, ['eps' + tag])

    def rstd(self, xt, xkey, i):
        b = self.b
        st = self.st[i % 2]
        k = ('nst' + self.tag, i % 2)
        b.act(self.junk[:], xt, AF.Square, [xkey], ['junk' + self.tag, k], accum_out=st[:, 0:1])
        b.act(st[:, 1:2], st[:, 0:1], AF.Ln, [k, 'eps' + self.tag], [k], scale=1.0 / DM, bias=self.eps[:, 0:1])
        b.act(st[:, 2:3], st[:, 1:2], AF.Exp, [k], [k], scale=-0.5)
        return st[:, 2:3], k


def build_norm_launch():
    nc = bass.Bass("TRN2", target_bir_lowering=False)
    x = nc.dram_tensor("x", [TOWN, DM], F32, kind="ExternalInput").ap()
    nw = nc.dram_tensor("nw", [1, DM], F32, kind="ExternalInput").ap()
    idn = nc.dram_tensor("idn", [128, 128], BF16, kind="ExternalInput").ap()
    hT = nc.dram_tensor("hT", [DM, TOWN], BF16, kind="ExternalOutput").ap()
    b = B(nc)
    idb = b.sb("idb", [128, 128], BF16)
    b.dma(idb[:], idn, [], ['idb'], sem='ld_id')
    ns = NormStage(b, nw, idb, "a")
    xt = [b.sb("xt%d" % i, [128, DM], F32) for i in range(2)]
    hb = [b.sb("hb%d" % i, [128, DM], BF16) for i in range(2)]
    hTt = [b.sb("hTt%d" % i, [128, 8, 512], BF16) for i in range(2)]
    PT = [b.ps("PT%d" % i, [128, 1024], BF16) for i in range(2)]
    hTv = hT.rearrange("(k p) t -> p k t", p=128)
    for t in range(TOWN // 128):
        s = t % 2
        b.dma(xt[s][:], x[t * 128:(t + 1) * 128, :], [], [('xt', s)], sem='ld_x%d' % s)
        r, rk = ns.rstd(xt[s][:], ('xt', s), t)
        b.stt(hb[s][:], xt[s][:], r, ns.nw[:], ALU.mult, ALU.mult, [('xt', s), rk, 'nwa'], [('hb', s)])
        for k in range(8):
            b.tr(PT[s][:, k * 128:(k + 1) * 128], hb[s][:, k * 128:(k + 1) * 128], idb[:], [('hb', s), 'idb'], [('PT', s)])
        g = t // 4
        gs = g % 2
        b.cp('dve' if t % 2 else 'act', hTt[gs][:, :, (t % 4) * 128:(t % 4 + 1) * 128],
             PT[s][:].rearrange("p (k n) -> p k n", k=8), [('PT', s)], [('hTt', gs, t % 4)])
        if t % 4 == 3:
            b.dma(hTv[:, :, g * 512:(g + 1) * 512], hTt[gs][:], [('hTt', gs, q) for q in range(4)], [], sem='st_h%d' % gs)
    b.finish()
    return nc


DBG = dict(ntt=16, parts='zvqrt')


def build_mixer_launch(layer, phases=(0, 1, 2, 3), stage=9):
    nc = bass.Bass("TRN2", target_bir_lowering=False)
    hT = nc.dram_tensor("hT", [DM, SEQ], BF16, kind="ExternalInput").ap()
    wm = nc.dram_tensor("wm", [4, DM, 512], F32, kind="ExternalInput").ap()
    lamv = nc.dram_tensor("lamv", [1, 256], F32, kind="ExternalInput").ap()
    subw = nc.dram_tensor("subw", [128, 1], F32, kind="ExternalInput").ap()
    idn = nc.dram_tensor("idn", [128, 128], BF16, kind="ExternalInput").ap()
    mka = nc.dram_tensor("mka", [128, 256], BF16, kind="ExternalInput").ap()
    mkb = nc.dram_tensor("mkb", [128, 128], BF16, kind="ExternalInput").ap()
    cosd = nc.dram_tensor("cosd", [128, 512], F32, kind="ExternalInput").ap()
    sind = nc.dram_tensor("sind", [128, 512], F32, kind="ExternalInput").ap()
    yT = nc.dram_tensor("yT", [512, SEQ], BF16, kind="ExternalOutput").ap()
    lam_init = lambda_init_value(layer)
    b = B(nc)
    idb = b.sb("idb", [128, 128], BF16)
    maskA = b.sb("maskA", [128, 256], BF16)
    maskB = b.sb("maskB", [128, 128], BF16)
    cosT = b.sb("cosT", [128, 64, 8], F32)
    sinT = b.sb("sinT", [128, 64, 8], F32)
    ones = b.sb("ones", [128, 128], BF16)
    onesS = b.sb("onesS", [128, 128], BF16)
    B1 = b.sb("B1", [64, 128], F32)
    B2 = b.sb("B2", [64, 128], F32)
    epsb = b.sb("epsb", [128, 1], F32)
    lt = b.sb("lt", [128, 256], F32)
    lj = b.sb("lj", [128, 128], F32)
    ls = b.sb("ls", [128, 8], F32)
    sw = b.sb("sw", [128, 2], F32)
    b.dma(idb[:], idn, [], ['idb'], sem='ld_c0')
    b.dma(maskA[:], mka, [], ['maskA'], sem='ld_c1')
    b.dma(maskB[:], mkb, [], ['maskB'], sem='ld_c2')
    b.dma(cosT[:].rearrange("p t i -> p (t i)"), cosd, [], ['cosT'], sem='ld_c3')
    b.dma(sinT[:].rearrange("p t i -> p (t i)"), sind, [], ['sinT'], sem='ld_c4')
    b.dma(lt[:], bcast_rows(lamv, 128), [], ['lt'], sem='ld_c5')
    b.dma(sw[:, 0:1], subw, [], ['sw0'], sem='ld_c6')
    b.memset('pool', ones[:], 1.0, ['ones'])
    b.memset('pool', onesS[:], 1.0 / 128, ['onesS'])
    b.memset('pool', B1[:], 0.0, ['B1'])
    b.memset('pool', B2[:], 0.0, ['B2'])
    b.memset('pool', B1[0:32, :], 1.0 / 32, ['B1'])
    b.memset('pool', B2[32:64, :], 1.0 / 32, ['B2'])
    b.memset('pool', epsb[:], SUBLN_EPS, ['epsb'])
    b.tt('dve', lj[:, 0:64], lt[:, 0:64], lt[:, 64:128], ALU.mult, ['lt'], ['lj'])
    b.tt('dve', lj[:, 64:128], lt[:, 128:192], lt[:, 192:256], ALU.mult, ['lt'], ['lj'])
    b.ts('dve', lt[:, 0:64], lj[:, 0:64], 1.0, ALU.mult, ['lj'], ['lt', 'ls0'], op1=ALU.add, accum_out=ls[:, 0:1])
    b.ts('dve', lt[:, 64:128], lj[:, 64:128], 1.0, ALU.mult, ['lj'], ['lt', 'ls1'], op1=ALU.add, accum_out=ls[:, 1:2])
    b.act(ls[:, 2:4], ls[:, 0:2], AF.Exp, ['ls0', 'ls1'], ['ls2'])
    b.tt('dve', ls[:, 4:5], ls[:, 3:4], ls[:, 2:3], ALU.subtract, ['ls2'], ['ls4'])
    b.ts('dve', ls[:, 5:6], ls[:, 4:5], -lam_init, ALU.add, ['ls4'], ['neglam'])
    b.ts('dve', sw[:, 1:2], sw[:, 0:1], 1.0 - lam_init, ALU.mult, ['sw0'], ['sw1'])
    neglam = ls[:, 5:6]
    wst = [b.sb("wst%d" % i, [128, 512], F32) for i in range(2)]
    wph = b.sb("wph", [128, 8, 512], BF16)
    hTt = [b.sb("hTt%d" % i, [128, 8, 512], BF16) for i in range(2)]
    qT = b.sb("qT", [128, SEQ], BF16)
    kT = b.sb("kT", [128, SEQ], BF16)
    zs = b.sb("zs", [128, SEQ], BF16)
    vT = b.sb("vT", [128, SEQ], BF16)
    acc = b.sb("acc", [128, SEQ], F32)
    V = b.sb("V", [128, 64, 128], BF16)
    Pb = [b.sb("Pb%d" % i, [128, 2, 512], BF16) for i in range(3)]
    stg = [b.sb("stg%d" % i, [128, 256], BF16) for i in range(4)]
    rt = [b.sb("rt%d" % i, [128, 4, 4, 8], F32) for i in range(2)]
    o1s = b.sb("o1s", [128, 512], F32)
    o2s = b.sb("o2s", [128, 512], F32)
    dens = b.sb("dens", [64, 512], F32)
    r1 = b.sb("r1", [128, 512], F32)
    r2 = b.sb("r2", [128, 512], F32)
    ob = b.sb("ob", [128, 512], F32)
    sq = b.sb("sq", [128, 512], BF16)
    rs = b.sb("rs", [128, 512], F32)
    yb = [b.sb("yb%d" % i, [128, 512], BF16) for i in range(2)]
    rr = b.sb("rr", [128, 1024], F32)
    ya = [b.sb("ya%d" % i, [128, 1024], BF16) for i in range(2)]
    PS = [b.ps("PS%d" % i, [128, 1024], F32) for i in range(4)]

    def bank(i):
        return PS[i // 2][:, (i % 2) * 512:(i % 2 + 1) * 512]

    def bank_bf(i):
        return bank(i).bitcast(BF16)

    hTv = hT.rearrange("(k p) t -> p k t", p=128)
    nload = [0]

    def load_h(tt):
        s = nload[0] % 2
        nload[0] += 1
        b.dma(hTt[s][:], hTv[:, :, tt * 512:(tt + 1) * 512], [], [('hTt', s)], sem='ld_h%d' % s)
        return s

    for ph in phases:
        isB = ph < 2
        for k in range(8):
            s = k % 2
            b.dma(wst[s][:], wm[ph, k * 128:(k + 1) * 128, :], [], [('wst', s)], sem='ld_w%d' % s)
            b.cp('pool', wph[:, k, :], wst[s][:], [('wst', s)], [('wph', k)])
        wkeys = [('wph', k) for k in range(8)]
        hs = load_h(0)
        pend = None
        for tt in range(DBG['ntt']):
            cur = hs
            if tt + 1 < DBG['ntt']:
                hs = load_h(tt + 1)
            hk = ('hTt', cur)
            for k in range(8 if 'z' in DBG['parts'] else 0):
                b.mm(bank(DBG.get('zb', 0)), wph[:, k, 384:512], hTt[cur][:, k, :], k == 0, k == 7, wkeys + [hk], [('bk', 0)])
            if 'z' in DBG['parts']:
                b.act(zs[:, tt * 512:(tt + 1) * 512], bank(DBG.get('zb', 0)), AF.Silu, [('bk', 0)], [('zs', tt)])
            for k in range(8 if 'v' in DBG['parts'] else 0):
                b.mm(bank(1), wph[:, k, 256:384], hTt[cur][:, k, :], k == 0, k == 7, wkeys + [hk], [('bk', 1)])
            if 'v' in DBG['parts']:
                b.cp('act', vT[:, tt * 512:(tt + 1) * 512], bank(1), [('bk', 1)], [('vT', tt)])
            pr = tt % 2
            for s4 in range(DBG.get('ns4', 4) if 'q' in DBG['parts'] else 0):
                qk = bank(2 + s4)[:, 0:256]
                qkk = ('bk', 2 + s4)
                for k in range(8):
                    b.mm(qk, hTt[cur][:, k, s4 * 128:(s4 + 1) * 128], wph[:, k, 0:256], k == 0, k == 7,
                         wkeys + [hk], [qkk])
                t = tt * 4 + s4
                qv = qk.rearrange("p (s d) -> p s d", s=4)
                sg = stg[s4]
                sgv = sg[:].rearrange("p (s d) -> p s d", s=4)
                sgk = ('stg', s4)
                R = rt[s4 % 2]
                rk = ('rt', s4 % 2)
                ca_ = cosT[:, t, :]
                cb = bass.AP(tensor=ca_.tensor, offset=ca_.offset, ap=[list(ca_.ap[0]), [0, 4], [1, 8]])
                sa_ = sinT[:, t, :]
                sb_ = bass.AP(tensor=sa_.tensor, offset=sa_.offset, ap=[list(sa_.ap[0]), [0, 4], [1, 8]])
                if 'r' not in DBG['parts']:
                    b.cp(DBG.get('qcp', 'act'), sg[:], qk, [qkk], [sgk])
                    continue
                t1 = qv[:, :, 0:8]
                t2 = qv[:, :, 8:16]
                b.tt('dve', R[:, 0, :, :], t1, cb, ALU.mult, [qkk, 'cosT'], [rk])
                b.tt('dve', R[:, 1, :, :], t2, sb_, ALU.mult, [qkk, 'sinT'], [rk])
                b.tt('dve', R[:, 2, :, :], t1, sb_, ALU.mult, [qkk, 'sinT'], [rk])
                b.tt('dve', R[:, 3, :, :], t2, cb, ALU.mult, [qkk, 'cosT'], [rk])
                b.tt('dve', sgv[:, :, 0:8], R[:, 0, :, :], R[:, 1, :, :], ALU.subtract, [rk], [sgk])
                b.tt('dve', sgv[:, :, 8:16], R[:, 2, :, :], R[:, 3, :, :], ALU.add, [rk], [sgk])
                b.cp('act', sgv[:, :, 16:64], qv[:, :, 16:64], [qkk], [sgk])
            tb = 6 + pr
            tbv = bank_bf(tb)
            if 't' not in DBG['parts']:
                continue
            for s4 in range(4):
                b.tr(tbv[:, s4 * 128:(s4 + 1) * 128], stg[s4][:, 0:128], idb[:], [('stg', s4), 'idb'], [('bk', tb)])
                b.tr(tbv[:, (4 + s4) * 128:(5 + s4) * 128], stg[s4][:, 128:256], idb[:], [('stg', s4), 'idb'], [('bk', tb)])
            b.cp('dve', qT[:, tt * 512:(tt + 1) * 512], tbv[:, 0:512], [('bk', tb)], [('qT', tt)])
            b.cp('dve', kT[:, tt * 512:(tt + 1) * 512], tbv[:, 512:1024], [('bk', tb)], [('kT', tt)])
        if stage < 2:
            continue
        allq = [('qT', i) for i in range(16)]
        allk = [('kT', i) for i in range(16)]
        allv = [('vT', i) for i in range(16)]
        allz = [('zs', i) for i in range(16)]
        if isB:
            for g8 in range(8):
                tb = 6 + g8 % 2
                tbv = bank_bf(tb)
                for j in range(8):
                    t = g8 * 8 + j
                    b.tr(tbv[:, j * 128:(j + 1) * 128], vT[:, t * 128:(t + 1) * 128], idb[:], allv + ['idb'], [('bk', tb)])
                b.cp('dve' if g8 % 2 else 'act', V[:, g8 * 8:(g8 + 1) * 8, :].rearrange("p a c -> p (a c)"), tbv[:, :],
                     [('bk', tb)], [('V', g8)])
            allV = [('V', i) for i in range(8)]
            if stage < 3:
                continue
            steps = []
            for qc in range(16):
                for kt in range(4 * qc + 4):
                    steps.append((qc, kt))
            LAG = 1

            def emit_S(n):
                qc, kt = steps[n]
                j = kt - 4 * qc
                c0 = 128 * j if j >= 0 else 0
                sp_ = n % 2
                for h2 in range(2):
                    Sb = bank(2 * sp_ + h2)
                    b.mm(Sb[:, c0:512], kT[64 * h2:64 * h2 + 64, kt * 128:(kt + 1) * 128],
                         qT[64 * h2:64 * h2 + 64, qc * 512 + c0:(qc + 1) * 512], True, j < 0,
                         allq + allk, [('bk', 2 * sp_ + h2)])
                if j >= 0:
                    for h2 in range(2):
                        Sb = bank(2 * sp_ + h2)
                        b.mm(Sb[:, c0:c0 + 128], idb[:], maskB[:], False, True, ['idb', 'maskB'], [('bk', 2 * sp_ + h2)])
                pb = Pb[n % 3]
                b.act(pb[:, :, c0:512], PS[sp_][:].rearrange("p (b n) -> p b n", b=2)[:, :, c0:512], AF.Exp,
                      [('bk', 2 * sp_), ('bk', 2 * sp_ + 1)], [('Pb', n % 3)], scale=0.125)

            def emit_AV(n):
                qc, kt = steps[n]
                j = kt - 4 * qc
                c0 = 128 * j if j >= 0 else 0
                first = kt == 0
                last = kt == 4 * qc + 3
                pb = Pb[n % 3]
                pk = [('Pb', n % 3)]
                b.mm(bank(4)[:, c0:512], V[:, kt, :], pb[:, 0, c0:512], first, last, allV + pk, [('bk', 4)])
                b.mm(bank(5)[:, c0:512], V[:, kt, :], pb[:, 1, c0:512], first, last, allV + pk, [('bk', 5)])
                b.mm(bank(6)[0:32, c0:512], ones[:, 0:32], pb[:, 0, c0:512], first, last, ['ones'] + pk, [('bk', 6)])
                b.mm(bank(6)[32:64, c0:512], ones[:, 0:32], pb[:, 1, c0:512], first, last, ['ones'] + pk, [('bk', 6)])
                if last:
                    epilogue(qc, n)

            pending = []

            def epilogue(qc, n):
                b.cp('dve', o1s[:], bank(4), [('bk', 4)], ['o1s'])
                b.cp('act', o2s[:], bank(5), [('bk', 5)], ['o2s'])
                b.cp('dve', dens[:], bank(6)[0:64, :], [('bk', 6)], ['dens'])

                def partB():
                    b.mm(bank(7), B1[:], dens[:], True, True, ['B1', 'dens'], [('bk', 7)])
                    b.recip(r1[:], bank(7), [('bk', 7)], ['r1'])
                    b.tt('dve', r1[:], o1s[:], r1[:], ALU.mult, ['o1s', 'r1'], ['r1'])

                def partC():
                    b.mm(bank(7), B2[:], dens[:], True, True, ['B2', 'dens'], [('bk', 7)])
                    b.recip(r2[:], bank(7), [('bk', 7)], ['r2'])
                    b.tt('dve', r2[:], o2s[:], r2[:], ALU.mult, ['o2s', 'r2'], ['r2'])
                    b.stt(ob[:], r2[:], neglam, r1[:], ALU.mult, ALU.add, ['r1', 'r2', 'neglam'], ['ob'])
                    b.tt('dve', sq[:], ob[:], ob[:], ALU.mult, ['ob'], ['sq'])

                def partD():
                    b.mm(bank(7), onesS[:], sq[:], True, True, ['onesS', 'sq'], [('bk', 7)])
                    b.act(rs[:], bank(7), AF.Ln, [('bk', 7), 'epsb'], ['rs'], bias=epsb[:, 0:1])
                    b.act(rs[:], rs[:], AF.Exp, ['rs'], ['rs'], scale=-0.5)
                    b.tt('dve', ob[:], ob[:], rs[:], ALU.mult, ['ob', 'rs'], ['ob'])
                    y = yb[qc % 2]
                    b.stt(y[:], ob[:], sw[:, 1:2], zs[:, qc * 512:(qc + 1) * 512], ALU.mult, ALU.mult,
                          ['ob', 'sw1'] + allz, [('yb', qc % 2)])
                    b.dma(yT[ph * 128:(ph + 1) * 128, qc * 512:(qc + 1) * 512], y[:], [('yb', qc % 2)], [],
                          sem='st_y%d' % (qc % 2))
                pending.append((n + 1, partB))
                pending.append((n + 2, partC))
                pending.append((n + 3, partD))

            for n in range(len(steps) + LAG):
                if n < len(steps):
                    emit_S(n)
                if n - LAG >= 0:
                    emit_AV(n - LAG)
                while pending and pending[0][0] <= n - LAG:
                    pending.pop(0)[1]()
            while pending:
                pending.pop(0)[1]()
        else:
            LAGA = 2
            for h in range(2):
                rb = 64 * h
                Vhs = [V[:, (32 * i):(32 * i + 32), :].rearrange("p a c -> p (a c)").rearrange("p (s d) -> p s d", d=64)
                       for i in range(2)]
                b.memset('pool', acc[:], 0.0, [('acc', gi, r_) for gi in range(64) for r_ in range(16)])
                blocks = []
                for di, D in enumerate((1, 4, 16)):
                    NCH = 64 // D
                    corder = list(range(0, NCH, 2)) + list(range(1, NCH, 2))
                    for c in corder:
                        for r_ in range(D):
                            blocks.append((di, D, c, r_))

                def build_V(di, D):
                    vi = (3 * h + di) % 2
                    Vh = Vhs[vi]
                    for g16 in range(4):
                        tb = 6 + g16 % 2
                        tbv = bank_bf(tb)
                        for j in range(16):
                            slot = g16 * 16 + j
                            c, r_ = slot // D, slot % D
                            tok0 = c * 128 * D + r_
                            b.tr(tbv[:, j * 64:(j + 1) * 64], vT[rb:rb + 64, tok0:tok0 + 127 * D + 1:D], idb[rb:rb + 64, rb:rb + 64],
                                 allv + ['idb'], [('bk', tb)])
                        b.cp('act', Vh[:, g16 * 16:(g16 + 1) * 16, :].rearrange("p s d -> p (s d)"), tbv[:, :],
                             [('bk', tb)], [('Vh', vi)] + [('V', g_) for g_ in range(8)])

                def emit_SA(n):
                    di, D, c, r_ = blocks[n]
                    NCH = 64 // D
                    tok0 = c * 128 * D + r_
                    nq = 256 if c + 1 < NCH else 128
                    Sb = bank(n % 4)
                    b.mm(Sb[:, 0:nq], kT[rb:rb + 64, tok0:tok0 + 127 * D + 1:D], qT[rb:rb + 64, tok0:tok0 + (nq - 1) * D + 1:D],
                         True, True, allq + allk, [('bk', n % 4)])
                    pb = Pb[n % 3]
                    b.act(pb[:, 0, 0:nq], Sb[:, 0:nq], AF.Exp, [('bk', n % 4)], [('Pb', n % 3)], scale=0.125)
                    b.tt('pool', pb[:, 0, 0:nq], pb[:, 0, 0:nq], maskA[:, 0:nq], ALU.mult, [('Pb', n % 3), 'maskA'], [('Pb', n % 3)])

                def emit_AVA(n):
                    di, D, c, r_ = blocks[n]
                    NCH = 64 // D
                    tok0 = c * 128 * D + r_
                    nq = 256 if c + 1 < NCH else 128
                    vi = (3 * h + di) % 2
                    Vh = Vhs[vi]
                    Ob = bank(4 + n % 2)
                    pb = Pb[n % 3]
                    slot = c * D + r_
                    b.mm(Ob[rb:rb + 64, 0:nq], Vh[:, slot, :], pb[:, 0, 0:nq], True, True,
                         [('Vh', vi), ('Pb', n % 3)], [('bk', 4 + n % 2)])
                    b.mm(Ob[64 - rb:128 - rb, 0:nq], ones[:, 0:64], pb[:, 0, 0:nq], True, True,
                         ['ones', ('Pb', n % 3)], [('bk', 4 + n % 2)])
                    g0 = tok0 // 128
                    ng = (nq * D) // 128
                    if D == 1:
                        keys = [('acc', g0 + gi, x_) for gi in range(ng) for x_ in range(16)]
                    elif D == 4:
                        keys = [('acc', g0 + gi, r_ + 4 * x_) for gi in range(ng) for x_ in range(4)]
                    else:
                        keys = [('acc', g0 + gi, r_) for gi in range(ng)]
                    av = acc[:, tok0:tok0 + (nq - 1) * D + 1:D]
                    b.tt('dve', av, Ob[:, 0:nq], av, ALU.add, [('bk', 4 + n % 2)] + keys, keys)

                build_V(0, 1)
                nb = len(blocks)
                for n in range(nb + LAGA):
                    if n < nb:
                        emit_SA(n)
                    m = n - LAGA
                    if m >= 0:
                        emit_AVA(m)
                        if m == 8:
                            build_V(1, 4)
                        if m == 64 + 8:
                            build_V(2, 16)
                allacc = [('acc', gi, x_) for gi in range(64) for x_ in range(16)]
                for pc in range(8):
                    cs = slice(pc * 1024, (pc + 1) * 1024)
                    y = ya[pc % 2]
                    b.recip(rr[rb:rb + 64, :], acc[64 - rb:128 - rb, cs], allacc, ['rr'])
                    b.tt('dve', rr[rb:rb + 64, :], acc[rb:rb + 64, cs], rr[rb:rb + 64, :], ALU.mult, allacc + ['rr'], ['rr'])
                    b.tt('dve', y[rb:rb + 64, :], rr[rb:rb + 64, :], zs[rb:rb + 64, cs], ALU.mult, ['rr'] + allz, [('ya', pc % 2)])
                    b.dma(yT[ph * 128 + rb:ph * 128 + rb + 64, cs], y[rb:rb + 64, :], [('ya', pc % 2)], [],
                          sem='st_a%d' % (pc % 2))
    b.finish()
    return nc


def build_merge_launch(last):
    nc = bass.Bass("TRN2", target_bir_lowering=False)
    yaT = nc.dram_tensor("yaT", [DM, TOWN], BF16, kind="ExternalInput").ap()
    ybT = nc.dram_tensor("ybT", [DM, TOWN], BF16, kind="ExternalInput").ap()
    hT = nc.dram_tensor("hT", [DM, TOWN], BF16, kind="ExternalInput").ap()
    x = nc.dram_tensor("x", [TOWN, DM], F32, kind="ExternalInput").ap()
    wp = nc.dram_tensor("wp", [5, DM, DM], F32, kind="ExternalInput").ap()
    nw = nc.dram_tensor("nw", [1, DM], F32, kind="ExternalInput").ap()
    idn = nc.dram_tensor("idn", [128, 128], BF16, kind="ExternalInput").ap()
    if last:
        out = nc.dram_tensor("out", [TOWN, DM], F32, kind="ExternalOutput").ap()
    else:
        xn_d = nc.dram_tensor("xn", [TOWN, DM], F32, kind="ExternalOutput").ap()
        hTn = nc.dram_tensor("hTn", [DM, TOWN], BF16, kind="ExternalOutput").ap()
    b = B(nc)
    idb = b.sb("idb", [128, 128], BF16)
    b.dma(idb[:], idn, [], ['idb'], sem='ld_id')
    ns = NormStage(b, nw, idb, "n")
    W = [b.sb("W%d" % i, [128, 8, DM], BF16) for i in range(5)]
    wst = [b.sb("wst%d" % i, [128, DM], F32) for i in range(2)]
    n = 0
    for wi in (2, 3, 0, 1, 4):
        for k in range(8):
            s = n % 2
            n += 1
            b.dma(wst[s][:], wp[wi, k * 128:(k + 1) * 128, :], [], [('wst', s)], sem='ld_w%d' % s)
            b.cp('pool', W[wi][:, k, :], wst[s][:], [('wst', s)], [('W', wi)])
    inT = [[b.sb("in%d_%d" % (i, s), [128, 8, 512], BF16) for s in range(1)] for i in range(3)]
    srcs = [yaT.rearrange("(k p) t -> p k t", p=128), ybT.rearrange("(k p) t -> p k t", p=128),
            hT.rearrange("(k p) t -> p k t", p=128)]
    sg = [b.sb("sg%d" % i, [128, 512], F32) for i in range(4)]
    m1 = [b.sb("m1_%d" % i, [128, 512], F32) for i in range(2)]
    m2 = [b.sb("m2_%d" % i, [128, 512], F32) for i in range(2)]
    mT = b.sb("mT", [128, 8, 512], BF16)
    xt = [b.sb("xt%d" % i, [128, DM], F32) for i in range(2)]
    xo = [b.sb("xo%d" % i, [128, DM], F32) for i in range(2)]
    hb = [b.sb("hb%d" % i, [128, DM], BF16) for i in range(2)]
    hTt = [b.sb("hTt%d" % i, [128, 8, 512], BF16) for i in range(2)]
    PS = [b.ps("PS%d" % i, [128, 512], F32) for i in range(7)]
    PT = b.ps("PT", [128, 1024], BF16)
    if not last:
        hTv = hTn.rearrange("(k p) t -> p k t", p=128)
    nx = 0
    for tq in range(TOWN // 512):
        s = 0
        for i in range(3):
            b.dma(inT[i][s][:], srcs[i][:, :, tq * 512:(tq + 1) * 512], [], [('in', i, s)], sem='ld_i%d_%d' % (i, s))
        for oc in range(8):
            pz = (oc % 2) * 2
            ocs = slice(oc * 128, (oc + 1) * 128)
            for k in range(8):
                b.mm(PS[pz][:], W[2][:, k, ocs], inT[2][s][:, k, :], k == 0, k == 7, [('W', 2), ('in', 2, s)], [('bk', pz)])
            for k in range(8):
                b.mm(PS[pz + 1][:], W[3][:, k, ocs], inT[2][s][:, k, :], k == 0, k == 7, [('W', 3), ('in', 2, s)], [('bk', pz + 1)])
            b.act(sg[pz][:], PS[pz][:], AF.Sigmoid, [('bk', pz)], [('sg', pz)])
            b.act(sg[pz + 1][:], PS[pz + 1][:], AF.Sigmoid, [('bk', pz + 1)], [('sg', pz + 1)])
            pp = 4 + (oc % 2)
            for k in range(8):
                b.mm(PS[pp][:], W[0][:, k, ocs], inT[0][s][:, k, :], k == 0, k == 7, [('W', 0), ('in', 0, s)], [('bk', pp)])
            b.tt('dve', m1[oc % 2][:], PS[pp][:], sg[pz][:], ALU.mult, [('bk', pp), ('sg', pz)], [('m1', oc % 2)])
            for k in range(8):
                b.mm(PS[6][:], W[1][:, k, ocs], inT[1][s][:, k, :], k == 0, k == 7, [('W', 1), ('in', 1, s)], [('bk', 6)])
            b.tt('dve', m2[oc % 2][:], PS[6][:], sg[pz + 1][:], ALU.mult, [('bk', 6), ('sg', pz + 1)], [('m2', oc % 2)])
            b.tt('pool', mT[:, oc, :], m1[oc % 2][:], m2[oc % 2][:], ALU.add, [('m1', oc % 2), ('m2', oc % 2)], [('mT', oc)])
        mkeys = [('mT', oc) for oc in range(8)]
        for s4 in range(4):
            t = tq * 4 + s4
            xs = nx % 2
            nx += 1
            b.dma(xt[xs][:], x[t * 128:(t + 1) * 128, :], [], [('xt', xs)], sem='ld_x%d' % xs)
            for half in range(2):
                pb_ = PS[half]
                for k in range(8):
                    b.mm(pb_[:], mT[:, k, s4 * 128:(s4 + 1) * 128], W[4][:, k, half * 512:(half + 1) * 512],
                         k == 0, k == 7, mkeys + [('W', 4)], [('bk', half)])
                b.tt('dve', xo[xs][:, half * 512:(half + 1) * 512], pb_[:], xt[xs][:, half * 512:(half + 1) * 512], ALU.add,
                     [('bk', half), ('xt', xs)], [('xo', xs, half)])
            xok = [('xo', xs, 0), ('xo', xs, 1)]
            if not last:
                b.dma(xn_d[t * 128:(t + 1) * 128, :], xo[xs][:], xok, [], sem='st_x%d' % xs)
            b.act(ns.junk[:], xo[xs][:], AF.Square, xok, ['junkn', ('nst', t % 2)], accum_out=ns.st[t % 2][:, 0:1])
            st_ = ns.st[t % 2]
            kk = ('nst', t % 2)
            b.act(st_[:, 1:2], st_[:, 0:1], AF.Ln, [kk, 'epsn'], [kk], scale=1.0 / DM, bias=ns.eps[:, 0:1])
            b.act(st_[:, 2:3], st_[:, 1:2], AF.Exp, [kk], [kk], scale=-0.5)
            if last:
                b.stt(xt[xs][:], xo[xs][:], st_[:, 2:3], ns.nw[:], ALU.mult, ALU.mult, xok + [kk, 'nwn', ('xt', xs)], [('xt', xs)])
                b.dma(out[t * 128:(t + 1) * 128, :], xt[xs][:], [('xt', xs)], [], sem='st_o%d' % xs)
            else:
                b.stt(hb[xs][:], xo[xs][:], st_[:, 2:3], ns.nw[:], ALU.mult, ALU.mult, xok + [kk, 'nwn'], [('hb', xs)])
                for k in range(8):
                    b.tr(PT[:, k * 128:(k + 1) * 128], hb[xs][:, k * 128:(k + 1) * 128], idb[:], [('hb', xs), 'idb'], ['PT'])
                gs = tq % 2
                b.cp('act', hTt[gs][:, :, s4 * 128:(s4 + 1) * 128], PT[:].rearrange("p (k n) -> p k n", k=8), ['PT'],
                     [('hTt', gs, s4)])
                if s4 == 3:
                    b.dma(hTv[:, :, tq * 512:(tq + 1) * 512], hTt[gs][:], [('hTt', gs, q) for q in range(4)], [],
                          sem='st_h%d' % gs)
    b.finish()
    return nc


def _consts():
    idn = np.eye(128, dtype=np.float32).astype(ml_dtypes.bfloat16)
    ki = np.arange(128)[:, None]
    qi = np.arange(128)[None, :]
    mka = np.concatenate([np.where(ki <= qi, 1.0, 0.0), np.where(ki >= qi, 1.0, 0.0)], axis=1).astype(np.float32)
    mkb = np.where(ki <= qi, 0.0, NEG).astype(np.float32)
    inv = (1.0 / (np.float32(500000.0) ** (np.arange(0, 16, 2, dtype=np.float32) / np.float32(16)))).astype(np.float32)
    pos = np.arange(SEQ, dtype=np.float32)
    ang = (pos[:, None] * inv[None, :]).astype(np.float32)
    cos = np.cos(ang).astype(np.float32).reshape(64, 128, 8).transpose(1, 0, 2).reshape(128, 512)
    sin = np.sin(ang).astype(np.float32).reshape(64, 128, 8).transpose(1, 0, 2).reshape(128, 512)
    return dict(idn=idn, mka=mka.astype(ml_dtypes.bfloat16), mkb=mkb.astype(ml_dtypes.bfloat16),
                cosd=np.ascontiguousarray(cos), sind=np.ascontiguousarray(sin))


def _mixer_weights(w_in_l, g):
    cols = []
    for j in range(2):
        hb = 2 * g + j
        for nm in ('qB', 'kB', 'vB', 'zB'):
            cols.append(np.arange(OFF[nm] + hb * 128, OFF[nm] + hb * 128 + 128))
    for j in range(2):
        ha = 4 * g + 2 * j
        for nm in ('qA', 'kA', 'vA', 'zA'):
            cols.append(np.arange(OFF[nm] + ha * 64, OFF[nm] + ha * 64 + 128))
    cols = np.concatenate(cols)
    w = w_in_l[:, cols]
    return np.ascontiguousarray(w.reshape(DM, 4, 512).transpose(1, 0, 2))


_NC_CACHE = {}


def _get(name, fn, *a):
    key = (name,) + a
    if key not in _NC_CACHE:
        _NC_CACHE[key] = fn(*a)
    return _NC_CACHE[key]


def kernel(x, norm_w, w_in, lambda_q1, lambda_k1, lambda_q2, lambda_k2, subln_w,
           w_proj_a, w_proj_b, w_out, final_norm_w):
    x = np.asarray(x, dtype=np.float32)
    cst = _consts()
    cores = list(range(8))
    xs = [np.ascontiguousarray(x[c // 4, (c % 4) * TOWN:(c % 4 + 1) * TOWN, :]) for c in cores]
    nca = build_norm_launch()
    res = run_bass_kernel_spmd(nca, [dict(x=xs[c], nw=np.ascontiguousarray(norm_w[0][None, :]), idn=cst['idn'])
                                     for c in cores], core_ids=cores)
    hown = [res.results[c]['hT'] for c in cores]
    for l in range(DEPTH):
        hfull = [np.ascontiguousarray(np.concatenate(hown[4 * bb:4 * bb + 4], axis=1)) for bb in range(2)]
        lamv = np.ascontiguousarray(np.concatenate([lambda_q1[l], lambda_k1[l], lambda_q2[l], lambda_k2[l]])[None, :].astype(np.float32))
        subw = np.ascontiguousarray(np.asarray(subln_w[l], dtype=np.float32).reshape(128, 1))
        ncm = build_mixer_launch(l)
        in_maps = []
        for c in cores:
            in_maps.append(dict(hT=hfull[c // 4], wm=_mixer_weights(np.asarray(w_in[l]), c % 4), lamv=lamv, subw=subw,
                                idn=cst['idn'], mka=cst['mka'], mkb=cst['mkb'], cosd=cst['cosd'], sind=cst['sind']))
        res = run_bass_kernel_spmd(ncm, in_maps, core_ids=cores)
        yTs = [res.results[c]['yT'] for c in cores]
        last = l == DEPTH - 1
        ncp = build_merge_launch(last)
        wp = np.ascontiguousarray(np.stack([np.asarray(w_proj_a[l]), np.asarray(w_proj_b[l]),
                                            np.asarray(w_in[l][:, OFF['ga']:OFF['ga'] + DM]),
                                            np.asarray(w_in[l][:, OFF['gb']:OFF['gb'] + DM]),
                                            np.asarray(w_out[l])]).astype(np.float32))
        nwn = np.ascontiguousarray((final_norm_w if last else norm_w[l + 1])[None, :].astype(np.float32))
        in_maps = []
        for c in cores:
            bb, g = c // 4, c % 4
            ts = slice(g * TOWN, (g + 1) * TOWN)
            yaT = np.ascontiguousarray(np.concatenate([yTs[4 * bb + gg][256:512, ts] for gg in range(4)], axis=0))
            ybT = np.ascontiguousarray(np.concatenate([yTs[4 * bb + gg][0:256, ts] for gg in range(4)], axis=0))
            in_maps.append(dict(yaT=yaT, ybT=ybT, hT=hown[c], x=xs[c], wp=wp, nw=nwn, idn=cst['idn']))
        res = run_bass_kernel_spmd(ncp, in_maps, core_ids=cores)
        if last:
            outs = [res.results[c]['out'] for c in cores]
        else:
            xs = [res.results[c]['xn'] for c in cores]
            hown = [res.results[c]['hTn'] for c in cores]
    out = np.empty((2, SEQ, DM), dtype=np.float32)
    for c in cores:
        out[c // 4, (c % 4) * TOWN:(c % 4 + 1) * TOWN, :] = outs[c]
    return out
```

```python
import math
import contextlib
import numpy as np
import ml_dtypes
import concourse.bass as bass
import concourse.mybir as mybir
from concourse.bass_utils import run_bass_kernel_spmd

F32 = mybir.dt.float32
BF16 = mybir.dt.bfloat16
AF = mybir.ActivationFunctionType
ALU = mybir.AluOpType

SEQ = 8192
DM = 1024
TOWN = 2048
DEPTH = 2
NEG = -30000.0
RMS_EPS = 1e-6
SUBLN_EPS = 1e-5
OFF = dict(qA=0, kA=1024, vA=2048, zA=3072, qB=4096, kB=5120, vB=6144, zB=7168, ga=8192, gb=9216)


def lambda_init_value(layer):
    return 0.8 - 0.6 * math.exp(-0.3 * layer)


class Prog:
    def __init__(self, nc):
        self.nc = nc
        self.ops = []

    def add(self, eng, fn, reads=(), writes=(), sem=None):
        self.ops.append(dict(eng=eng, fn=fn, reads=tuple(reads), writes=tuple(writes),
                             chan=(sem if sem is not None else eng), dma=sem is not None))

    def emit(self):
        nc = self.nc
        ops = self.ops
        last_w = {}
        readers = {}
        deps = [None] * len(ops)
        needed = [False] * len(ops)
        for i, op in enumerate(ops):
            d = set()
            for r in op['reads']:
                if r in last_w:
                    d.add(last_w[r])
            for w in op['writes']:
                if w in last_w:
                    d.add(last_w[w])
                for j in readers.get(w, ()):
                    d.add(j)
            d.discard(i)
            for r in op['reads']:
                readers.setdefault(r, []).append(i)
            for w in op['writes']:
                last_w[w] = i
                readers[w] = []
            dd = set()
            for j in d:
                pj = ops[j]
                if (not pj['dma']) and pj['eng'] == op['eng'] and op['eng'] == 'pe' and not op['dma']:
                    continue
                dd.add(j)
            deps[i] = dd
            for j in dd:
                needed[j] = True
        chans = []
        for op in ops:
            if op['chan'] not in chans:
                chans.append(op['chan'])
        count = {c: 0 for c in chans}
        value = [0] * len(ops)
        last_dma = {}
        for i, op in enumerate(ops):
            if op['dma']:
                needed[i] = True
                last_dma[op['chan']] = i
            if needed[i]:
                count[op['chan']] += 16 if op['dma'] else 1
                value[i] = count[op['chan']]
        sems = {}
        with contextlib.ExitStack() as st:
            for n, c in enumerate(chans):
                sems[c] = st.enter_context(nc.semaphore("s%d" % n))
            block = st.enter_context(nc.Block())
            deco = dict(pe=block.tensor, act=block.scalar, dve=block.vector,
                        pool=block.gpsimd, sp=block.sync)
            for e in ['pe', 'act', 'dve', 'pool', 'sp']:
                my = [i for i, op in enumerate(ops) if op['eng'] == e]
                if not my and e != 'sp':
                    continue

                def body(engine, my=my, e=e):
                    waited = {}
                    for i in my:
                        op = ops[i]
                        need = {}
                        for j in deps[i]:
                            c = ops[j]['chan']
                            need[c] = max(need.get(c, 0), value[j])
                        for c, v in need.items():
                            if waited.get(c, 0) >= v:
                                continue
                            engine.wait_ge(sems[c], v)
                            waited[c] = v
                        ins = op['fn'](engine)
                        if needed[i]:
                            ins.then_inc(sems[op['chan']], 16 if op['dma'] else 1)
                    if e == 'sp':
                        for c, i in last_dma.items():
                            if waited.get(c, 0) < value[i]:
                                engine.wait_ge(sems[c], value[i])
                deco[e](body)


class B:
    def __init__(self, nc):
        self.nc = nc
        self.p = Prog(nc)
        self.st = contextlib.ExitStack()
        self.nload = 0

    def sb(self, name, shape, dt):
        return self.st.enter_context(self.nc.sbuf_tensor(name, shape, dt))

    def ps(self, name, shape, dt):
        return self.st.enter_context(self.nc.psum_tensor(name, shape, dt))

    def mm(self, out, lhsT, rhs, start, stop, r, w):
        self.p.add('pe', lambda e: e.matmul(out=out, lhsT=lhsT, rhs=rhs, start=start, stop=stop), r, w)

    def tr(self, out, in_, ident, r, w):
        self.p.add('pe', lambda e: e.transpose(out=out, in_=in_, identity=ident), r, w)

    def act(self, out, in_, func, r, w, scale=None, bias=None, accum_out=None):
        kw = {}
        if scale is not None:
            kw['scale'] = scale
        if bias is not None:
            kw['bias'] = bias
        if accum_out is not None:
            kw['accum_out'] = accum_out
        self.p.add('act', lambda e: e.activation(out=out, in_=in_, func=func, **kw), r, w)

    def tt(self, eng, out, in0, in1, op, r, w):
        self.p.add(eng, lambda e: e.tensor_tensor(out=out, in0=in0, in1=in1, op=op), r, w)

    def ts(self, eng, out, in0, s1, op0, r, w, s2=None, op1=None, accum_out=None):
        kw = {}
        if op1 is not None:
            kw['op1'] = op1
        if accum_out is not None:
            kw['accum_out'] = accum_out
        self.p.add(eng, lambda e: e.tensor_scalar(out=out, in0=in0, scalar1=s1, scalar2=s2, op0=op0, **kw), r, w)

    def stt(self, out, in0, scalar, in1, op0, op1, r, w):
        self.p.add('dve', lambda e: e.scalar_tensor_tensor(out=out, in0=in0, scalar=scalar, in1=in1,
                                                            op0=op0, op1=op1), r, w)

    def cp(self, eng, out, in_, r, w):
        if eng == 'act':
            self.p.add('act', lambda e: e.activation(out=out, in_=in_, func=AF.Copy), r, w)
        else:
            self.p.add(eng, lambda e: e.tensor_copy(out=out, in_=in_), r, w)

    def recip(self, out, in_, r, w):
        self.p.add('dve', lambda e: e.reciprocal(out=out, in_=in_), r, w)

    def memset(self, eng, ap, val, w):
        self.p.add(eng, lambda e: e.memset(ap, val), (), w)

    def dma(self, out, in_, r, w, sem, eng='sp'):
        self.p.add(eng, lambda e: e.dma_start(out=out, in_=in_), r, w, sem=sem)

    def finish(self):
        self.p.emit()
        self.st.close()


def bcast_rows(ap2d, nparts):
    return bass.AP(tensor=ap2d.tensor, offset=ap2d.offset, ap=[[0, nparts]] + [list(x) for x in ap2d.ap[1:]])


class NormStage:
    def __init__(self, b, nw_dram, idb, tag):
        self.b = b
        self.tag = tag
        self.nw = b.sb("nw" + tag, [128, DM], F32)
        self.eps = b.sb("eps" + tag, [128, 1], F32)
        self.junk = b.sb("junk" + tag, [128, DM], BF16)
        self.st = [b.sb("nst%s%d" % (tag, i), [128, 4], F32) for i in range(2)]
        self.idb = idb
        b.dma(self.nw[:], bcast_rows(nw_dram, 128), [], ['nw' + tag], sem='ld_nw' + tag)
        b.memset('pool', self.eps[:], RMS_EPS, ['eps' + tag])

    def rstd(self, xt, xkey, i):
        b = self.b
        st = self.st[i % 2]
        k = ('nst' + self.tag, i % 2)
        b.act(self.junk[:], xt, AF.Square, [xkey], ['junk' + self.tag, k], accum_out=st[:, 0:1])
        b.act(st[:, 1:2], st[:, 0:1], AF.Ln, [k, 'eps' + self.tag], [k], scale=1.0 / DM, bias=self.eps[:, 0:1])
        b.act(st[:, 2:3], st[:, 1:2], AF.Exp, [k], [k], scale=-0.5)
        return st[:, 2:3], k


def build_norm_launch():
    nc = bass.Bass("TRN2", target_bir_lowering=False)
    x = nc.dram_tensor("x", [TOWN, DM], F32, kind="ExternalInput").ap()
    nw = nc.dram_tensor("nw", [1, DM], F32, kind="ExternalInput").ap()
    idn = nc.dram_tensor("idn", [128, 128], BF16, kind="ExternalInput").ap()
    hT = nc.dram_tensor("hT", [DM, TOWN], BF16, kind="ExternalOutput").ap()
    b = B(nc)
    idb = b.sb("idb", [128, 128], BF16)
    b.dma(idb[:], idn, [], ['idb'], sem='ld_id')
    ns = NormStage(b, nw, idb, "a")
    xt = [b.sb("xt%d" % i, [128, DM], F32) for i in range(2)]
    hb = [b.sb("hb%d" % i, [128, DM], BF16) for i in range(2)]
    hTt = [b.sb("hTt%d" % i, [128, 8, 512], BF16) for i in range(2)]
    PT = [b.ps("PT%d" % i, [128, 1024], BF16) for i in range(2)]
    hTv = hT.rearrange("(k p) t -> p k t", p=128)
    for t in range(TOWN // 128):
        s = t % 2
        b.dma(xt[s][:], x[t * 128:(t + 1) * 128, :], [], [('xt', s)], sem='ld_x%d' % s)
        r, rk = ns.rstd(xt[s][:], ('xt', s), t)
        b.stt(hb[s][:], xt[s][:], r, ns.nw[:], ALU.mult, ALU.mult, [('xt', s), rk, 'nwa'], [('hb', s)])
        for k in range(8):
            b.tr(PT[s][:, k * 128:(k + 1) * 128], hb[s][:, k * 128:(k + 1) * 128], idb[:], [('hb', s), 'idb'], [('PT', s)])
        g = t // 4
        gs = g % 2
        b.cp('dve' if t % 2 else 'act', hTt[gs][:, :, (t % 4) * 128:(t % 4 + 1) * 128],
             PT[s][:].rearrange("p (k n) -> p k n", k=8), [('PT', s)], [('hTt', gs, t % 4)])
        if t % 4 == 3:
            b.dma(hTv[:, :, g * 512:(g + 1) * 512], hTt[gs][:], [('hTt', gs, q) for q in range(4)], [], sem='st_h%d' % gs)
    b.finish()
    return nc


DBG = dict(ntt=16, parts='zvqrt')


def build_mixer_launch(layer, phases=(0, 1, 2, 3), stage=9):
    nc = bass.Bass("TRN2", target_bir_lowering=False)
    hT = nc.dram_tensor("hT", [DM, SEQ], BF16, kind="ExternalInput").ap()
    wm = nc.dram_tensor("wm", [4, DM, 512], F32, kind="ExternalInput").ap()
    lamv = nc.dram_tensor("lamv", [1, 256], F32, kind="ExternalInput").ap()
    subw = nc.dram_tensor("subw", [128, 1], F32, kind="ExternalInput").ap()
    idn = nc.dram_tensor("idn", [128, 128], BF16, kind="ExternalInput").ap()
    mka = nc.dram_tensor("mka", [128, 256], BF16, kind="ExternalInput").ap()
    mkb = nc.dram_tensor("mkb", [128, 128], BF16, kind="ExternalInput").ap()
    cosd = nc.dram_tensor("cosd", [128, 512], F32, kind="ExternalInput").ap()
    sind = nc.dram_tensor("sind", [128, 512], F32, kind="ExternalInput").ap()
    yT = nc.dram_tensor("yT", [512, SEQ], BF16, kind="ExternalOutput").ap()
    lam_init = lambda_init_value(layer)
    b = B(nc)
    idb = b.sb("idb", [128, 128], BF16)
    maskA = b.sb("maskA", [128, 256], BF16)
    maskB = b.sb("maskB", [128, 128], BF16)
    cosT = b.sb("cosT", [128, 64, 8], F32)
    sinT = b.sb("sinT", [128, 64, 8], F32)
    ones = b.sb("ones", [128, 128], BF16)
    onesS = b.sb("onesS", [128, 128], BF16)
    B1 = b.sb("B1", [64, 128], F32)
    B2 = b.sb("B2", [64, 128], F32)
    epsb = b.sb("epsb", [128, 1], F32)
    lt = b.sb("lt", [128, 256], F32)
    lj = b.sb("lj", [128, 128], F32)
    ls = b.sb("ls", [128, 8], F32)
    sw = b.sb("sw", [128, 2], F32)
    b.dma(idb[:], idn, [], ['idb'], sem='ld_c0')
    b.dma(maskA[:], mka, [], ['maskA'], sem='ld_c1')
    b.dma(maskB[:], mkb, [], ['maskB'], sem='ld_c2')
    b.dma(cosT[:].rearrange("p t i -> p (t i)"), cosd, [], ['cosT'], sem='ld_c3')
    b.dma(sinT[:].rearrange("p t i -> p (t i)"), sind, [], ['sinT'], sem='ld_c4')
    b.dma(lt[:], bcast_rows(lamv, 128), [], ['lt'], sem='ld_c5')
    b.dma(sw[:, 0:1], subw, [], ['sw0'], sem='ld_c6')
    b.memset('pool', ones[:], 1.0, ['ones'])
    b.memset('pool', onesS[:], 1.0 / 128, ['onesS'])
    b.memset('pool', B1[:], 0.0, ['B1'])
    b.memset('pool', B2[:], 0.0, ['B2'])
    b.memset('pool', B1[0:32, :], 1.0 / 32, ['B1'])
    b.memset('pool', B2[32:64, :], 1.0 / 32, ['B2'])
    b.memset('pool', epsb[:], SUBLN_EPS, ['epsb'])
    b.tt('dve', lj[:, 0:64], lt[:, 0:64], lt[:, 64:128], ALU.mult, ['lt'], ['lj'])
    b.tt('dve', lj[:, 64:128], lt[:, 128:192], lt[:, 192:256], ALU.mult, ['lt'], ['lj'])
    b.ts('dve', lt[:, 0:64], lj[:, 0:64], 1.0, ALU.mult, ['lj'], ['lt', 'ls0'], op1=ALU.add, accum_out=ls[:, 0:1])
    b.ts('dve', lt[:, 64:128], lj[:, 64:128], 1.0, ALU.mult, ['lj'], ['lt', 'ls1'], op1=ALU.add, accum_out=ls[:, 1:2])
    b.act(ls[:, 2:4], ls[:, 0:2], AF.Exp, ['ls0', 'ls1'], ['ls2'])
    b.tt('dve', ls[:, 4:5], ls[:, 3:4], ls[:, 2:3], ALU.subtract, ['ls2'], ['ls4'])
    b.ts('dve', ls[:, 5:6], ls[:, 4:5], -lam_init, ALU.add, ['ls4'], ['neglam'])
    b.ts('dve', sw[:, 1:2], sw[:, 0:1], 1.0 - lam_init, ALU.mult, ['sw0'], ['sw1'])
    neglam = ls[:, 5:6]
    wst = [b.sb("wst%d" % i, [128, 512], F32) for i in range(2)]
    wph = b.sb("wph", [128, 8, 512], BF16)
    hTt = [b.sb("hTt%d" % i, [128, 8, 512], BF16) for i in range(2)]
    qT = b.sb("qT", [128, SEQ], BF16)
    kT = b.sb("kT", [128, SEQ], BF16)
    zs = b.sb("zs", [128, SEQ], BF16)
    vT = b.sb("vT", [128, SEQ], BF16)
    acc = b.sb("acc", [128, SEQ], F32)
    qD = {4: b.sb("q4", [128, SEQ], BF16), 16: b.sb("q16", [128, SEQ], BF16)}
    V = b.sb("V", [128, 64, 128], BF16)
    Pb = [b.sb("Pb%d" % i, [128, 2, 512], BF16) for i in range(3)]
    stg = [b.sb("stg%d" % i, [128, 256], BF16) for i in range(8)]
    rt = [b.sb("rt%d" % i, [128, 4, 4, 8], F32) for i in range(2)]
    o1s = b.sb("o1s", [128, 512], F32)
    o2s = b.sb("o2s", [128, 512], F32)
    dens = b.sb("dens", [64, 512], F32)
    r1 = b.sb("r1", [128, 512], F32)
    ob = b.sb("ob", [128, 512], F32)
    sq = b.sb("sq", [128, 512], BF16)
    yb = [b.sb("yb%d" % i, [128, 512], BF16) for i in range(2)]
    rr = b.sb("rr", [128, 512], F32)
    ya = yb
    PS = [b.ps("PS%d" % i, [128, 1024], F32) for i in range(4)]

    def bank(i):
        return PS[i // 2][:, (i % 2) * 512:(i % 2 + 1) * 512]

    def bank_bf(i):
        return bank(i).bitcast(BF16)

    hTv = hT.rearrange("(k p) t -> p k t", p=128)
    nload = [0]

    def load_h(tt):
        s = nload[0] % 2
        nload[0] += 1
        b.dma(hTt[s][:], hTv[:, :, tt * 512:(tt + 1) * 512], [], [('hTt', s)], sem='ld_h%d' % s)
        return s

    for ph in phases:
        isB = ph < 2
        for k in range(8):
            s = k % 2
            b.dma(wst[s][:], wm[ph, k * 128:(k + 1) * 128, :], [], [('wst', s)], sem='ld_w%d' % s)
            b.cp('pool', wph[:, k, :], wst[s][:], [('wst', s)], [('wph', k)])
        wkeys = [('wph', k) for k in range(8)]
        hs = load_h(0)
        pendT = []
        for tt in range(DBG['ntt']):
            cur = hs
            if tt + 1 < DBG['ntt']:
                hs = load_h(tt + 1)
            hk = ('hTt', cur)
            for k in range(8 if 'z' in DBG['parts'] else 0):
                b.mm(bank(DBG.get('zb', 0)), wph[:, k, 384:512], hTt[cur][:, k, :], k == 0, k == 7, wkeys + [hk], [('bk', 0)])
            if 'z' in DBG['parts']:
                b.act(zs[:, tt * 512:(tt + 1) * 512], bank(DBG.get('zb', 0)), AF.Silu, [('bk', 0)], [('zs', tt)])
            for k in range(8 if 'v' in DBG['parts'] else 0):
                b.mm(bank(1), wph[:, k, 256:384], hTt[cur][:, k, :], k == 0, k == 7, wkeys + [hk], [('bk', 1)])
            if 'v' in DBG['parts']:
                b.cp('act', vT[:, tt * 512:(tt + 1) * 512], bank(1), [('bk', 1)], [('vT', tt)])
            while pendT:
                pendT.pop(0)()
            pr = tt % 2
            for s4 in range(DBG.get('ns4', 4) if 'q' in DBG['parts'] else 0):
                qk = bank(2 + s4)[:, 0:256]
                qkk = ('bk', 2 + s4)
                for k in range(8):
                    b.mm(qk, hTt[cur][:, k, s4 * 128:(s4 + 1) * 128], wph[:, k, 0:256], k == 0, k == 7,
                         wkeys + [hk], [qkk])
                t = tt * 4 + s4
                qv = qk.rearrange("p (s d) -> p s d", s=4)
                sg = stg[pr * 4 + s4]
                sgv = sg[:].rearrange("p (s d) -> p s d", s=4)
                sgk = ('stg', pr * 4 + s4)
                R = rt[s4 % 2]
                rk = ('rt', s4 % 2)
                ca_ = cosT[:, t, :]
                cb = bass.AP(tensor=ca_.tensor, offset=ca_.offset, ap=[list(ca_.ap[0]), [0, 4], [1, 8]])
                sa_ = sinT[:, t, :]
                sb_ = bass.AP(tensor=sa_.tensor, offset=sa_.offset, ap=[list(sa_.ap[0]), [0, 4], [1, 8]])
                if 'r' not in DBG['parts']:
                    b.cp(DBG.get('qcp', 'act'), sg[:], qk, [qkk], [sgk])
                    continue
                t1 = qv[:, :, 0:8]
                t2 = qv[:, :, 8:16]
                b.tt('dve', R[:, 0, :, :], t1, cb, ALU.mult, [qkk, 'cosT'], [rk])
                b.tt('dve', R[:, 1, :, :], t2, sb_, ALU.mult, [qkk, 'sinT'], [rk])
                b.tt('dve', R[:, 2, :, :], t1, sb_, ALU.mult, [qkk, 'sinT'], [rk])
                b.tt('dve', R[:, 3, :, :], t2, cb, ALU.mult, [qkk, 'cosT'], [rk])
                b.tt('dve', sgv[:, :, 0:8], R[:, 0, :, :], R[:, 1, :, :], ALU.subtract, [rk], [sgk])
                b.tt('dve', sgv[:, :, 8:16], R[:, 2, :, :], R[:, 3, :, :], ALU.add, [rk], [sgk])
                b.cp('act', sgv[:, :, 16:64], qv[:, :, 16:64], [qkk], [sgk])
            def do_tr(tt=tt, pr=pr):
                tb = 6 + pr
                tbv = bank_bf(tb)
                for s4 in range(4):
                    sl_ = pr * 4 + s4
                    b.tr(tbv[:, s4 * 128:(s4 + 1) * 128], stg[sl_][:, 0:128], idb[:], [('stg', sl_), 'idb'], [('bk', tb)])
                    b.tr(tbv[:, (4 + s4) * 128:(5 + s4) * 128], stg[sl_][:, 128:256], idb[:], [('stg', sl_), 'idb'], [('bk', tb)])
                b.cp('dve', qT[:, tt * 512:(tt + 1) * 512], tbv[:, 0:512], [('bk', tb)], [('qT', tt)])
                b.cp('dve', kT[:, tt * 512:(tt + 1) * 512], tbv[:, 512:1024], [('bk', tb)], [('kT', tt)])
            pendT.append(do_tr)
        while pendT:
            pendT.pop(0)()
        if stage < 2:
            continue
        allq = [('qT', i) for i in range(16)]
        allk = [('kT', i) for i in range(16)]
        allv = [('vT', i) for i in range(16)]
        allz = [('zs', i) for i in range(16)]
        if isB:
            for g8 in range(8):
                tb = 6 + g8 % 2
                tbv = bank_bf(tb)
                for j in range(8):
                    t = g8 * 8 + j
                    b.tr(tbv[:, j * 128:(j + 1) * 128], vT[:, t * 128:(t + 1) * 128], idb[:], allv + ['idb'], [('bk', tb)])
                b.cp('dve' if g8 % 2 else 'act', V[:, g8 * 8:(g8 + 1) * 8, :].rearrange("p a c -> p (a c)"), tbv[:, :],
                     [('bk', tb)], [('V', g8)])
            allV = [('V', i) for i in range(8)]
            if stage < 3:
                continue
            steps = []
            for qc in range(16):
                for kt in range(4 * qc + 4):
                    steps.append((qc, kt))
            LAG = 1

            def emit_S(n):
                qc, kt = steps[n]
                j = kt - 4 * qc
                c0 = 128 * j if j >= 0 else 0
                sp_ = n % 2
                for h2 in range(2):
                    Sb = bank(2 * sp_ + h2)
                    b.mm(Sb[:, c0:512], kT[64 * h2:64 * h2 + 64, kt * 128:(kt + 1) * 128],
                         qT[64 * h2:64 * h2 + 64, qc * 512 + c0:(qc + 1) * 512], True, j < 0,
                         allq + allk, [('bk', 2 * sp_ + h2)])
                if j >= 0:
                    for h2 in range(2):
                        Sb = bank(2 * sp_ + h2)
                        b.mm(Sb[:, c0:c0 + 128], idb[:], maskB[:], False, True, ['idb', 'maskB'], [('bk', 2 * sp_ + h2)])
                pb = Pb[n % 3]
                b.act(pb[:, :, c0:512], PS[sp_][:].rearrange("p (b n) -> p b n", b=2)[:, :, c0:512], AF.Exp,
                      [('bk', 2 * sp_), ('bk', 2 * sp_ + 1)], [('Pb', n % 3)], scale=0.125)

            def emit_AV(n):
                qc, kt = steps[n]
                j = kt - 4 * qc
                c0 = 128 * j if j >= 0 else 0
                first = kt == 0
                last = kt == 4 * qc + 3
                pb = Pb[n % 3]
                pk = [('Pb', n % 3)]
                b.mm(bank(4)[:, c0:512], V[:, kt, :], pb[:, 0, c0:512], first, last, allV + pk, [('bk', 4)])
                b.mm(bank(5)[:, c0:512], V[:, kt, :], pb[:, 1, c0:512], first, last, allV + pk, [('bk', 5)])
                b.mm(bank(6)[0:32, c0:512], ones[:, 0:32], pb[:, 0, c0:512], first, last, ['ones'] + pk, [('bk', 6)])
                b.mm(bank(6)[32:64, c0:512], ones[:, 0:32], pb[:, 1, c0:512], first, last, ['ones'] + pk, [('bk', 6)])
                if last:
                    epilogue(qc, n)

            pending = []

            def epilogue(qc, n):
                b.cp('dve', o1s[:], bank(4), [('bk', 4)], ['o1s'])
                b.cp('act', o2s[:], bank(5), [('bk', 5)], ['o2s'])
                b.cp('dve', dens[:], bank(6)[0:64, :], [('bk', 6)], ['dens'])

                def partB():
                    b.mm(bank(7), B1[:], dens[:], True, True, ['B1', 'dens'], [('bk', 7)])
                    b.recip(r1[:], bank(7), [('bk', 7)], ['r1'])
                    b.tt('dve', o1s[:], o1s[:], r1[:], ALU.mult, ['o1s', 'r1'], ['o1s'])

                def partC():
                    b.mm(bank(7), B2[:], dens[:], True, True, ['B2', 'dens'], [('bk', 7)])
                    b.recip(r1[:], bank(7), [('bk', 7)], ['r1'])
                    b.tt('dve', o2s[:], o2s[:], r1[:], ALU.mult, ['o2s', 'r1'], ['o2s'])
                    b.stt(ob[:], o2s[:], neglam, o1s[:], ALU.mult, ALU.add, ['o1s', 'o2s', 'neglam'], ['ob'])
                    b.tt('dve', sq[:], ob[:], ob[:], ALU.mult, ['ob'], ['sq'])

                def partD():
                    b.mm(bank(7), onesS[:], sq[:], True, True, ['onesS', 'sq'], [('bk', 7)])
                    b.act(r1[:], bank(7), AF.Ln, [('bk', 7), 'epsb'], ['r1'], bias=epsb[:, 0:1])
                    b.act(r1[:], r1[:], AF.Exp, ['r1'], ['r1'], scale=-0.5)
                    b.tt('dve', ob[:], ob[:], r1[:], ALU.mult, ['ob', 'r1'], ['ob'])
                    y = yb[qc % 2]
                    b.stt(y[:], ob[:], sw[:, 1:2], zs[:, qc * 512:(qc + 1) * 512], ALU.mult, ALU.mult,
                          ['ob', 'sw1'] + allz, [('yb', qc % 2)])
                    b.dma(yT[ph * 128:(ph + 1) * 128, qc * 512:(qc + 1) * 512], y[:], [('yb', qc % 2)], [],
                          sem='st_y%d' % (qc % 2))
                pending.append((n + 1, partB))
                pending.append((n + 2, partC))
                pending.append((n + 3, partD))

            for n in range(len(steps) + LAG):
                if n < len(steps):
                    emit_S(n)
                if n - LAG >= 0:
                    emit_AV(n - LAG)
                while pending and pending[0][0] <= n - LAG:
                    pending.pop(0)[1]()
            while pending:
                pending.pop(0)[1]()
        else:
            LAGA = 3
            PbA = [Pb[i // 2][:, i % 2, 0:256] for i in range(6)]
            for h in range(2):
                rb = 64 * h
                Vhs = [V[:, (32 * i):(32 * i + 32), :].rearrange("p a c -> p (a c)").rearrange("p (s d) -> p s d", d=64)
                       for i in range(2)]
                b.memset('pool', acc[:], 0.0, [('acc', gi, r_) for gi in range(64) for r_ in range(16)])
                blocks = []
                for di, D in enumerate((1, 4, 16)):
                    NCH = 64 // D
                    corder = (list(range(0, NCH, 2)) + list(range(1, NCH, 2))) if D < 16 else list(range(NCH))
                    for c in corder:
                        for r_ in range(D):
                            blocks.append((di, D, c, r_))

                def build_V(di, D):
                    vi = (3 * h + di) % 2
                    Vh = Vhs[vi]
                    for g16 in range(4):
                        tb = 6 + g16 % 2
                        tbv = bank_bf(tb)
                        for j in range(16):
                            slot = g16 * 16 + j
                            c, r_ = slot // D, slot % D
                            tok0 = c * 128 * D + r_
                            b.tr(tbv[:, j * 64:(j + 1) * 64], vT[rb:rb + 64, tok0:tok0 + 127 * D + 1:D], idb[rb:rb + 64, rb:rb + 64],
                                 allv + ['idb'], [('bk', tb)])
                        b.cp('act', Vh[:, g16 * 16:(g16 + 1) * 16, :].rearrange("p s d -> p (s d)"), tbv[:, :],
                             [('bk', tb)], [('Vh', vi)] + [('V', g_) for g_ in range(8)])

                def emit_SA(n):
                    di, D, c, r_ = blocks[n]
                    NCH = 64 // D
                    tok0 = c * 128 * D + r_
                    nq = 256 if c + 1 < NCH else 128
                    Sb = bank(n % 4)
                    if D == 1:
                        qsrc = qT[rb:rb + 64, tok0:tok0 + nq]
                        qkeys = allq
                    else:
                        q0 = (r_ * NCH + c) * 128
                        qsrc = qD[D][rb:rb + 64, q0:q0 + nq]
                        qkeys = [('qD', D)]
                    b.mm(Sb[:, 0:nq], kT[rb:rb + 64, tok0:tok0 + 127 * D + 1:D], qsrc,
                         True, True, qkeys + allk, [('bk', n % 4)])
                    pb = PbA[n % 6]
                    b.act(pb[:, 0:nq], Sb[:, 0:nq], AF.Exp, [('bk', n % 4)], [('PbA', n % 6)], scale=0.125)
                    b.tt('pool', pb[:, 0:nq], pb[:, 0:nq], maskA[:, 0:nq], ALU.mult, [('PbA', n % 6), 'maskA'], [('PbA', n % 6)])

                def emit_AVA(n):
                    di, D, c, r_ = blocks[n]
                    NCH = 64 // D
                    tok0 = c * 128 * D + r_
                    nq = 256 if c + 1 < NCH else 128
                    vi = (3 * h + di) % 2
                    Vh = Vhs[vi]
                    Ob = bank(4 + n % 2)
                    pb = PbA[n % 6]
                    slot = c * D + r_
                    b.mm(Ob[rb:rb + 64, 0:nq], Vh[:, slot, :], pb[:, 0:nq], True, True,
                         [('Vh', vi), ('PbA', n % 6)], [('bk', 4 + n % 2)])
                    b.mm(Ob[64 - rb:128 - rb, 0:nq], ones[:, 0:64], pb[:, 0:nq], True, True,
                         ['ones', ('PbA', n % 6)], [('bk', 4 + n % 2)])
                    g0 = tok0 // 128
                    ng = (nq * D) // 128
                    if D == 1:
                        keys = [('acc', g0 + gi, x_) for gi in range(ng) for x_ in range(16)]
                    elif D == 4:
                        keys = [('acc', g0 + gi, r_ + 4 * x_) for gi in range(ng) for x_ in range(4)]
                    else:
                        keys = [('acc', g0 + gi, r_) for gi in range(ng)]
                    av = acc[:, tok0:tok0 + (nq - 1) * D + 1:D]
                    b.tt('dve', av, Ob[:, 0:nq], av, ALU.add, [('bk', 4 + n % 2)] + keys, keys)

                def destride(D, eng):
                    NCH = 64 // D
                    src = qT[:].rearrange("p (c i r) -> p r c i", r=D, i=128)
                    dst = qD[D][:].rearrange("p (r c i) -> p r c i", r=D, i=128)
                    for r_ in range(D):
                        e_ = eng[r_ % len(eng)]
                        b.cp(e_, dst[:, r_, :, :], src[:, r_, :, :], allq, [('qD', D)])

                def normalise(cidx):
                    keys_c = [('acc', gi, x_) for gi in range(16 * cidx, 16 * cidx + 16) for x_ in range(16)]
                    for pc in range(4 * cidx, 4 * cidx + 4):
                        cs = slice(pc * 512, (pc + 1) * 512)
                        y = ya[pc % 2]
                        b.recip(rr[rb:rb + 64, :], acc[64 - rb:128 - rb, cs], keys_c, ['rr'])
                        b.tt('dve', rr[rb:rb + 64, :], acc[rb:rb + 64, cs], rr[rb:rb + 64, :], ALU.mult, keys_c + ['rr'], ['rr'])
                        b.tt('dve', y[rb:rb + 64, :], rr[rb:rb + 64, :], zs[rb:rb + 64, cs], ALU.mult, ['rr'] + allz, [('yb', pc % 2)])
                        b.dma(yT[ph * 128 + rb:ph * 128 + rb + 64, cs], y[rb:rb + 64, :], [('yb', pc % 2)], [],
                              sem='st_a%d' % (pc % 2))

                build_V(0, 1)
                nb = len(blocks)
                for n in range(nb + LAGA):
                    if n < nb:
                        emit_SA(n)
                    m = n - LAGA
                    if m >= 0:
                        emit_AVA(m)
                        if m == 4 and h == 0:
                            destride(4, ['act'])
                        if m == 30 and h == 0:
                            destride(16, ['act'])
                        if m == 8:
                            build_V(1, 4)
                        if m == 64 + 8:
                            build_V(2, 16)
                        if blocks[m][1] == 16 and blocks[m][3] == 15:
                            normalise(blocks[m][2])
    b.finish()
    return nc


def build_merge_launch(last):
    nc = bass.Bass("TRN2", target_bir_lowering=False)
    yaT = nc.dram_tensor("yaT", [DM, TOWN], BF16, kind="ExternalInput").ap()
    ybT = nc.dram_tensor("ybT", [DM, TOWN], BF16, kind="ExternalInput").ap()
    hT = nc.dram_tensor("hT", [DM, TOWN], BF16, kind="ExternalInput").ap()
    x = nc.dram_tensor("x", [TOWN, DM], F32, kind="ExternalInput").ap()
    wp = nc.dram_tensor("wp", [5, DM, DM], F32, kind="ExternalInput").ap()
    nw = nc.dram_tensor("nw", [1, DM], F32, kind="ExternalInput").ap()
    idn = nc.dram_tensor("idn", [128, 128], BF16, kind="ExternalInput").ap()
    if last:
        out = nc.dram_tensor("out", [TOWN, DM], F32, kind="ExternalOutput").ap()
    else:
        xn_d = nc.dram_tensor("xn", [TOWN, DM], F32, kind="ExternalOutput").ap()
        hTn = nc.dram_tensor("hTn", [DM, TOWN], BF16, kind="ExternalOutput").ap()
    b = B(nc)
    idb = b.sb("idb", [128, 128], BF16)
    b.dma(idb[:], idn, [], ['idb'], sem='ld_id')
    ns = NormStage(b, nw, idb, "n")
    W = [b.sb("W%d" % i, [128, 8, DM], BF16) for i in range(5)]
    wst = [b.sb("wst%d" % i, [128, DM], F32) for i in range(2)]
    n = 0
    for wi in (2, 3, 0, 1, 4):
        for k in range(8):
            s = n % 2
            n += 1
            b.dma(wst[s][:], wp[wi, k * 128:(k + 1) * 128, :], [], [('wst', s)], sem='ld_w%d' % s)
            b.cp('pool', W[wi][:, k, :], wst[s][:], [('wst', s)], [('W', wi)])
    inT = [[b.sb("in%d_%d" % (i, s), [128, 8, 512], BF16) for s in range(1)] for i in range(3)]
    srcs = [yaT.rearrange("(k p) t -> p k t", p=128), ybT.rearrange("(k p) t -> p k t", p=128),
            hT.rearrange("(k p) t -> p k t", p=128)]
    sg = [b.sb("sg%d" % i, [128, 512], F32) for i in range(4)]
    m1 = [b.sb("m1_%d" % i, [128, 512], F32) for i in range(2)]
    m2 = [b.sb("m2_%d" % i, [128, 512], F32) for i in range(2)]
    mT = b.sb("mT", [128, 8, 512], BF16)
    xt = [b.sb("xt%d" % i, [128, DM], F32) for i in range(2)]
    xo = [b.sb("xo%d" % i, [128, DM], F32) for i in range(2)]
    hb = [b.sb("hb%d" % i, [128, DM], BF16) for i in range(2)]
    hTt = [b.sb("hTt%d" % i, [128, 8, 512], BF16) for i in range(2)]
    PS = [b.ps("PS%d" % i, [128, 512], F32) for i in range(7)]
    PT = b.ps("PT", [128, 1024], BF16)
    if not last:
        hTv = hTn.rearrange("(k p) t -> p k t", p=128)
    nx = 0
    for tq in range(TOWN // 512):
        s = 0
        for i in range(3):
            b.dma(inT[i][s][:], srcs[i][:, :, tq * 512:(tq + 1) * 512], [], [('in', i, s)], sem='ld_i%d_%d' % (i, s))
        for oc in range(8):
            pz = (oc % 2) * 2
            ocs = slice(oc * 128, (oc + 1) * 128)
            for k in range(8):
                b.mm(PS[pz][:], W[2][:, k, ocs], inT[2][s][:, k, :], k == 0, k == 7, [('W', 2), ('in', 2, s)], [('bk', pz)])
            for k in range(8):
                b.mm(PS[pz + 1][:], W[3][:, k, ocs], inT[2][s][:, k, :], k == 0, k == 7, [('W', 3), ('in', 2, s)], [('bk', pz + 1)])
            b.act(sg[pz][:], PS[pz][:], AF.Sigmoid, [('bk', pz)], [('sg', pz)])
            b.act(sg[pz + 1][:], PS[pz + 1][:], AF.Sigmoid, [('bk', pz + 1)], [('sg', pz + 1)])
            pp = 4 + (oc % 2)
            for k in range(8):
                b.mm(PS[pp][:], W[0][:, k, ocs], inT[0][s][:, k, :], k == 0, k == 7, [('W', 0), ('in', 0, s)], [('bk', pp)])
            b.tt('dve', m1[oc % 2][:], PS[pp][:], sg[pz][:], ALU.mult, [('bk', pp), ('sg', pz)], [('m1', oc % 2)])
            for k in range(8):
                b.mm(PS[6][:], W[1][:, k, ocs], inT[1][s][:, k, :], k == 0, k == 7, [('W', 1), ('in', 1, s)], [('bk', 6)])
            b.tt('dve', m2[oc % 2][:], PS[6][:], sg[pz + 1][:], ALU.mult, [('bk', 6), ('sg', pz + 1)], [('m2', oc % 2)])
            b.tt('pool', mT[:, oc, :], m1[oc % 2][:], m2[oc % 2][:], ALU.add, [('m1', oc % 2), ('m2', oc % 2)], [('mT', oc)])
        mkeys = [('mT', oc) for oc in range(8)]
        for s4 in range(4):
            t = tq * 4 + s4
            xs = nx % 2
            nx += 1
            b.dma(xt[xs][:], x[t * 128:(t + 1) * 128, :], [], [('xt', xs)], sem='ld_x%d' % xs)
            for half in range(2):
                pb_ = PS[half]
                for k in range(8):
                    b.mm(pb_[:], mT[:, k, s4 * 128:(s4 + 1) * 128], W[4][:, k, half * 512:(half + 1) * 512],
                         k == 0, k == 7, mkeys + [('W', 4)], [('bk', half)])
                b.tt('dve', xo[xs][:, half * 512:(half + 1) * 512], pb_[:], xt[xs][:, half * 512:(half + 1) * 512], ALU.add,
                     [('bk', half), ('xt', xs)], [('xo', xs, half)])
            xok = [('xo', xs, 0), ('xo', xs, 1)]
            if not last:
                b.dma(xn_d[t * 128:(t + 1) * 128, :], xo[xs][:], xok, [], sem='st_x%d' % xs)
            b.act(ns.junk[:], xo[xs][:], AF.Square, xok, ['junkn', ('nst', t % 2)], accum_out=ns.st[t % 2][:, 0:1])
            st_ = ns.st[t % 2]
            kk = ('nst', t % 2)
            b.act(st_[:, 1:2], st_[:, 0:1], AF.Ln, [kk, 'epsn'], [kk], scale=1.0 / DM, bias=ns.eps[:, 0:1])
            b.act(st_[:, 2:3], st_[:, 1:2], AF.Exp, [kk], [kk], scale=-0.5)
            if last:
                b.stt(xt[xs][:], xo[xs][:], st_[:, 2:3], ns.nw[:], ALU.mult, ALU.mult, xok + [kk, 'nwn', ('xt', xs)], [('xt', xs)])
                b.dma(out[t * 128:(t + 1) * 128, :], xt[xs][:], [('xt', xs)], [], sem='st_o%d' % xs)
            else:
                b.stt(hb[xs][:], xo[xs][:], st_[:, 2:3], ns.nw[:], ALU.mult, ALU.mult, xok + [kk, 'nwn'], [('hb', xs)])
                for k in range(8):
                    b.tr(PT[:, k * 128:(k + 1) * 128], hb[xs][:, k * 128:(k + 1) * 128], idb[:], [('hb', xs), 'idb'], ['PT'])
                gs = tq % 2
                b.cp('act', hTt[gs][:, :, s4 * 128:(s4 + 1) * 128], PT[:].rearrange("p (k n) -> p k n", k=8), ['PT'],
                     [('hTt', gs, s4)])
                if s4 == 3:
                    b.dma(hTv[:, :, tq * 512:(tq + 1) * 512], hTt[gs][:], [('hTt', gs, q) for q in range(4)], [],
                          sem='st_h%d' % gs)
    b.finish()
    return nc


def _consts():
    idn = np.eye(128, dtype=np.float32).astype(ml_dtypes.bfloat16)
    ki = np.arange(128)[:, None]
    qi = np.arange(128)[None, :]
    mka = np.concatenate([np.where(ki <= qi, 1.0, 0.0), np.where(ki >= qi, 1.0, 0.0)], axis=1).astype(np.float32)
    mkb = np.where(ki <= qi, 0.0, NEG).astype(np.float32)
    inv = (1.0 / (np.float32(500000.0) ** (np.arange(0, 16, 2, dtype=np.float32) / np.float32(16)))).astype(np.float32)
    pos = np.arange(SEQ, dtype=np.float32)
    ang = (pos[:, None] * inv[None, :]).astype(np.float32)
    cos = np.cos(ang).astype(np.float32).reshape(64, 128, 8).transpose(1, 0, 2).reshape(128, 512)
    sin = np.sin(ang).astype(np.float32).reshape(64, 128, 8).transpose(1, 0, 2).reshape(128, 512)
    return dict(idn=idn, mka=mka.astype(ml_dtypes.bfloat16), mkb=mkb.astype(ml_dtypes.bfloat16),
                cosd=np.ascontiguousarray(cos), sind=np.ascontiguousarray(sin))


def _mixer_weights(w_in_l, g):
    cols = []
    for j in range(2):
        hb = 2 * g + j
        for nm in ('qB', 'kB', 'vB', 'zB'):
            cols.append(np.arange(OFF[nm] + hb * 128, OFF[nm] + hb * 128 + 128))
    for j in range(2):
        ha = 4 * g + 2 * j
        for nm in ('qA', 'kA', 'vA', 'zA'):
            cols.append(np.arange(OFF[nm] + ha * 64, OFF[nm] + ha * 64 + 128))
    cols = np.concatenate(cols)
    w = w_in_l[:, cols]
    return np.ascontiguousarray(w.reshape(DM, 4, 512).transpose(1, 0, 2))


_NC_CACHE = {}


def _get(name, fn, *a):
    key = (name,) + a
    if key not in _NC_CACHE:
        _NC_CACHE[key] = fn(*a)
    return _NC_CACHE[key]


def kernel(x, norm_w, w_in, lambda_q1, lambda_k1, lambda_q2, lambda_k2, subln_w,
           w_proj_a, w_proj_b, w_out, final_norm_w):
    x = np.asarray(x, dtype=np.float32)
    cst = _consts()
    cores = list(range(8))
    xs = [np.ascontiguousarray(x[c // 4, (c % 4) * TOWN:(c % 4 + 1) * TOWN, :]) for c in cores]
    nca = build_norm_launch()
    res = run_bass_kernel_spmd(nca, [dict(x=xs[c], nw=np.ascontiguousarray(norm_w[0][None, :]), idn=cst['idn'])
                                     for c in cores], core_ids=cores)
    hown = [res.results[c]['hT'] for c in cores]
    for l in range(DEPTH):
        hfull = [np.ascontiguousarray(np.concatenate(hown[4 * bb:4 * bb + 4], axis=1)) for bb in range(2)]
        lamv = np.ascontiguousarray(np.concatenate([lambda_q1[l], lambda_k1[l], lambda_q2[l], lambda_k2[l]])[None, :].astype(np.float32))
        subw = np.ascontiguousarray(np.asarray(subln_w[l], dtype=np.float32).reshape(128, 1))
        ncm = build_mixer_launch(l)
        in_maps = []
        for c in cores:
            in_maps.append(dict(hT=hfull[c // 4], wm=_mixer_weights(np.asarray(w_in[l]), c % 4), lamv=lamv, subw=subw,
                                idn=cst['idn'], mka=cst['mka'], mkb=cst['mkb'], cosd=cst['cosd'], sind=cst['sind']))
        res = run_bass_kernel_spmd(ncm, in_maps, core_ids=cores)
        yTs = [res.results[c]['yT'] for c in cores]
        last = l == DEPTH - 1
        ncp = build_merge_launch(last)
        wp = np.ascontiguousarray(np.stack([np.asarray(w_proj_a[l]), np.asarray(w_proj_b[l]),
                                            np.asarray(w_in[l][:, OFF['ga']:OFF['ga'] + DM]),
                                            np.asarray(w_in[l][:, OFF['gb']:OFF['gb'] + DM]),
                                            np.asarray(w_out[l])]).astype(np.float32))
        nwn = np.ascontiguousarray((final_norm_w if last else norm_w[l + 1])[None, :].astype(np.float32))
        in_maps = []
        for c in cores:
            bb, g = c // 4, c % 4
            ts = slice(g * TOWN, (g + 1) * TOWN)
            yaT = np.ascontiguousarray(np.concatenate([yTs[4 * bb + gg][256:512, ts] for gg in range(4)], axis=0))
            ybT = np.ascontiguousarray(np.concatenate([yTs[4 * bb + gg][0:256, ts] for gg in range(4)], axis=0))
            in_maps.append(dict(yaT=yaT, ybT=ybT, hT=hown[c], x=xs[c], wp=wp, nw=nwn, idn=cst['idn']))
        res = run_bass_kernel_spmd(ncp, in_maps, core_ids=cores)
        if last:
            outs = [res.results[c]['out'] for c in cores]
        else:
            xs = [res.results[c]['xn'] for c in cores]
            hown = [res.results[c]['hTn'] for c in cores]
    out = np.empty((2, SEQ, DM), dtype=np.float32)
    for c in cores:
        out[c // 4, (c % 4) * TOWN:(c % 4 + 1) * TOWN, :] = outs[c]
    return out
```

```python
import math
import contextlib
import numpy as np
import ml_dtypes
import concourse.bass as bass
import concourse.mybir as mybir
from concourse.bass_utils import run_bass_kernel_spmd

F32 = mybir.dt.float32
BF16 = mybir.dt.bfloat16
AF = mybir.ActivationFunctionType
ALU = mybir.AluOpType

SEQ = 8192
DM = 1024
TOWN = 2048
DEPTH = 2
NEG = -30000.0
RMS_EPS = 1e-6
SUBLN_EPS = 1e-5
OFF = dict(qA=0, kA=1024, vA=2048, zA=3072, qB=4096, kB=5120, vB=6144, zB=7168, ga=8192, gb=9216)


def lambda_init_value(layer):
    return 0.8 - 0.6 * math.exp(-0.3 * layer)


class Prog:
    def __init__(self, nc):
        self.nc = nc
        self.ops = []

    def add(self, eng, fn, reads=(), writes=(), sem=None):
        self.ops.append(dict(eng=eng, fn=fn, reads=tuple(reads), writes=tuple(writes),
                             chan=(sem if sem is not None else eng), dma=sem is not None))

    def emit(self):
        nc = self.nc
        ops = self.ops
        last_w = {}
        readers = {}
        deps = [None] * len(ops)
        needed = [False] * len(ops)
        for i, op in enumerate(ops):
            d = set()
            for r in op['reads']:
                if r in last_w:
                    d.add(last_w[r])
            for w in op['writes']:
                if w in last_w:
                    d.add(last_w[w])
                for j in readers.get(w, ()):
                    d.add(j)
            d.discard(i)
            for r in op['reads']:
                readers.setdefault(r, []).append(i)
            for w in op['writes']:
                last_w[w] = i
                readers[w] = []
            dd = set()
            for j in d:
                pj = ops[j]
                if (not pj['dma']) and pj['eng'] == op['eng'] and op['eng'] == 'pe' and not op['dma']:
                    continue
                dd.add(j)
            deps[i] = dd
            for j in dd:
                needed[j] = True
        chans = []
        for op in ops:
            if op['chan'] not in chans:
                chans.append(op['chan'])
        count = {c: 0 for c in chans}
        value = [0] * len(ops)
        last_dma = {}
        for i, op in enumerate(ops):
            if op['dma']:
                needed[i] = True
                last_dma[op['chan']] = i
            if needed[i]:
                count[op['chan']] += 16 if op['dma'] else 1
                value[i] = count[op['chan']]
        sems = {}
        with contextlib.ExitStack() as st:
            for n, c in enumerate(chans):
                sems[c] = st.enter_context(nc.semaphore("s%d" % n))
            block = st.enter_context(nc.Block())
            deco = dict(pe=block.tensor, act=block.scalar, dve=block.vector,
                        pool=block.gpsimd, sp=block.sync)
            for e in ['pe', 'act', 'dve', 'pool', 'sp']:
                my = [i for i, op in enumerate(ops) if op['eng'] == e]
                if not my and e != 'sp':
                    continue

                def body(engine, my=my, e=e):
                    waited = {}
                    for i in my:
                        op = ops[i]
                        need = {}
                        for j in deps[i]:
                            c = ops[j]['chan']
                            need[c] = max(need.get(c, 0), value[j])
                        for c, v in need.items():
                            if waited.get(c, 0) >= v:
                                continue
                            engine.wait_ge(sems[c], v)
                            waited[c] = v
                        ins = op['fn'](engine)
                        if needed[i]:
                            ins.then_inc(sems[op['chan']], 16 if op['dma'] else 1)
                    if e == 'sp':
                        for c, i in last_dma.items():
                            if waited.get(c, 0) < value[i]:
                                engine.wait_ge(sems[c], value[i])
                deco[e](body)


class B:
    def __init__(self, nc):
        self.nc = nc
        self.p = Prog(nc)
        self.st = contextlib.ExitStack()
        self.nload = 0

    def sb(self, name, shape, dt):
        return self.st.enter_context(self.nc.sbuf_tensor(name, shape, dt))

    def ps(self, name, shape, dt):
        return self.st.enter_context(self.nc.psum_tensor(name, shape, dt))

    def mm(self, out, lhsT, rhs, start, stop, r, w):
        self.p.add('pe', lambda e: e.matmul(out=out, lhsT=lhsT, rhs=rhs, start=start, stop=stop), r, w)

    def tr(self, out, in_, ident, r, w):
        self.p.add('pe', lambda e: e.transpose(out=out, in_=in_, identity=ident), r, w)

    def act(self, out, in_, func, r, w, scale=None, bias=None, accum_out=None):
        kw = {}
        if scale is not None:
            kw['scale'] = scale
        if bias is not None:
            kw['bias'] = bias
        if accum_out is not None:
            kw['accum_out'] = accum_out
        self.p.add('act', lambda e: e.activation(out=out, in_=in_, func=func, **kw), r, w)

    def tt(self, eng, out, in0, in1, op, r, w):
        self.p.add(eng, lambda e: e.tensor_tensor(out=out, in0=in0, in1=in1, op=op), r, w)

    def ts(self, eng, out, in0, s1, op0, r, w, s2=None, op1=None, accum_out=None):
        kw = {}
        if op1 is not None:
            kw['op1'] = op1
        if accum_out is not None:
            kw['accum_out'] = accum_out
        self.p.add(eng, lambda e: e.tensor_scalar(out=out, in0=in0, scalar1=s1, scalar2=s2, op0=op0, **kw), r, w)

    def stt(self, out, in0, scalar, in1, op0, op1, r, w):
        self.p.add('dve', lambda e: e.scalar_tensor_tensor(out=out, in0=in0, scalar=scalar, in1=in1,
                                                            op0=op0, op1=op1), r, w)

    def cp(self, eng, out, in_, r, w):
        if eng == 'act':
            self.p.add('act', lambda e: e.activation(out=out, in_=in_, func=AF.Copy), r, w)
        else:
            self.p.add(eng, lambda e: e.tensor_copy(out=out, in_=in_), r, w)

    def recip(self, out, in_, r, w):
        self.p.add('dve', lambda e: e.reciprocal(out=out, in_=in_), r, w)

    def memset(self, eng, ap, val, w):
        self.p.add(eng, lambda e: e.memset(ap, val), (), w)

    def dma(self, out, in_, r, w, sem, eng='sp'):
        self.p.add(eng, lambda e: e.dma_start(out=out, in_=in_), r, w, sem=sem)

    def finish(self):
        self.p.emit()
        self.st.close()


def bcast_rows(ap2d, nparts):
    return bass.AP(tensor=ap2d.tensor, offset=ap2d.offset, ap=[[0, nparts]] + [list(x) for x in ap2d.ap[1:]])


class NormStage:
    def __init__(self, b, nw_dram, idb, tag):
        self.b = b
        self.tag = tag
        self.nw = b.sb("nw" + tag, [128, DM], F32)
        self.eps = b.sb("eps" + tag, [128, 1], F32)
        self.junk = b.sb("junk" + tag, [128, DM], BF16)
        self.st = [b.sb("nst%s%d" % (tag, i), [128, 4], F32) for i in range(2)]
        self.idb = idb
        b.dma(self.nw[:], bcast_rows(nw_dram, 128), [], ['nw' + tag], sem='ld_nw' + tag)
        b.memset('pool', self.eps[:], RMS_EPS, ['eps' + tag])

    def rstd(self, xt, xkey, i):
        b = self.b
        st = self.st[i % 2]
        k = ('nst' + self.tag, i % 2)
        b.act(self.junk[:], xt, AF.Square, [xkey], ['junk' + self.tag, k], accum_out=st[:, 0:1])
        b.act(st[:, 1:2], st[:, 0:1], AF.Ln, [k, 'eps' + self.tag], [k], scale=1.0 / DM, bias=self.eps[:, 0:1])
        b.act(st[:, 2:3], st[:, 1:2], AF.Exp, [k], [k], scale=-0.5)
        return st[:, 2:3], k


def build_norm_launch():
    nc = bass.Bass("TRN2", target_bir_lowering=False)
    x = nc.dram_tensor("x", [TOWN, DM], F32, kind="ExternalInput").ap()
    nw = nc.dram_tensor("nw", [1, DM], F32, kind="ExternalInput").ap()
    idn = nc.dram_tensor("idn", [128, 128], BF16, kind="ExternalInput").ap()
    hT = nc.dram_tensor("hT", [DM, TOWN], BF16, kind="ExternalOutput").ap()
    b = B(nc)
    idb = b.sb("idb", [128, 128], BF16)
    b.dma(idb[:], idn, [], ['idb'], sem='ld_id')
    ns = NormStage(b, nw, idb, "a")
    xt = [b.sb("xt%d" % i, [128, DM], F32) for i in range(2)]
    hb = [b.sb("hb%d" % i, [128, DM], BF16) for i in range(2)]
    hTt = [b.sb("hTt%d" % i, [128, 8, 512], BF16) for i in range(2)]
    PT = [b.ps("PT%d" % i, [128, 1024], BF16) for i in range(2)]
    hTv = hT.rearrange("(k p) t -> p k t", p=128)
    for t in range(TOWN // 128):
        s = t % 2
        b.dma(xt[s][:], x[t * 128:(t + 1) * 128, :], [], [('xt', s)], sem='ld_x%d' % s)
        r, rk = ns.rstd(xt[s][:], ('xt', s), t)
        b.stt(hb[s][:], xt[s][:], r, ns.nw[:], ALU.mult, ALU.mult, [('xt', s), rk, 'nwa'], [('hb', s)])
        for k in range(8):
            b.tr(PT[s][:, k * 128:(k + 1) * 128], hb[s][:, k * 128:(k + 1) * 128], idb[:], [('hb', s), 'idb'], [('PT', s)])
        g = t // 4
        gs = g % 2
        b.cp('dve' if t % 2 else 'act', hTt[gs][:, :, (t % 4) * 128:(t % 4 + 1) * 128],
             PT[s][:].rearrange("p (k n) -> p k n", k=8), [('PT', s)], [('hTt', gs, t % 4)])
        if t % 4 == 3:
            b.dma(hTv[:, :, g * 512:(g + 1) * 512], hTt[gs][:], [('hTt', gs, q) for q in range(4)], [], sem='st_h%d' % gs)
    b.finish()
    return nc


DBG = dict(ntt=16, parts='zvqrt')


def build_mixer_launch(layer, phases=(0, 1, 2, 3), stage=9):
    nc = bass.Bass("TRN2", target_bir_lowering=False)
    hT = nc.dram_tensor("hT", [DM, SEQ], BF16, kind="ExternalInput").ap()
    wm = nc.dram_tensor("wm", [4, DM, 512], F32, kind="ExternalInput").ap()
    lamv = nc.dram_tensor("lamv", [1, 256], F32, kind="ExternalInput").ap()
    subw = nc.dram_tensor("subw", [128, 1], F32, kind="ExternalInput").ap()
    idn = nc.dram_tensor("idn", [128, 128], BF16, kind="ExternalInput").ap()
    mka = nc.dram_tensor("mka", [128, 256], BF16, kind="ExternalInput").ap()
    mkb = nc.dram_tensor("mkb", [128, 128], BF16, kind="ExternalInput").ap()
    cosd = nc.dram_tensor("cosd", [128, 512], F32, kind="ExternalInput").ap()
    sind = nc.dram_tensor("sind", [128, 512], F32, kind="ExternalInput").ap()
    yT = nc.dram_tensor("yT", [512, SEQ], BF16, kind="ExternalOutput").ap()
    lam_init = lambda_init_value(layer)
    b = B(nc)
    idb = b.sb("idb", [128, 128], BF16)
    maskA = b.sb("maskA", [128, 256], BF16)
    maskB = b.sb("maskB", [128, 128], BF16)
    cosT = b.sb("cosT", [128, 64, 8], F32)
    sinT = b.sb("sinT", [128, 64, 8], F32)
    ones = b.sb("ones", [128, 128], BF16)
    onesS = b.sb("onesS", [128, 128], BF16)
    B1 = b.sb("B1", [64, 128], F32)
    B2 = b.sb("B2", [64, 128], F32)
    epsb = b.sb("epsb", [128, 1], F32)
    lt = b.sb("lt", [128, 256], F32)
    lj = b.sb("lj", [128, 128], F32)
    ls = b.sb("ls", [128, 8], F32)
    sw = b.sb("sw", [128, 2], F32)
    b.dma(idb[:], idn, [], ['idb'], sem='ld_c0')
    b.dma(maskA[:], mka, [], ['maskA'], sem='ld_c1')
    b.dma(maskB[:], mkb, [], ['maskB'], sem='ld_c2')
    b.dma(cosT[:].rearrange("p t i -> p (t i)"), cosd, [], ['cosT'], sem='ld_c3')
    b.dma(sinT[:].rearrange("p t i -> p (t i)"), sind, [], ['sinT'], sem='ld_c4')
    b.dma(lt[:], bcast_rows(lamv, 128), [], ['lt'], sem='ld_c5')
    b.dma(sw[:, 0:1], subw, [], ['sw0'], sem='ld_c6')
    b.memset('pool', ones[:], 1.0, ['ones'])
    b.memset('pool', onesS[:], 1.0 / 128, ['onesS'])
    b.memset('pool', B1[:], 0.0, ['B1'])
    b.memset('pool', B2[:], 0.0, ['B2'])
    b.memset('pool', B1[0:32, :], 1.0 / 32, ['B1'])
    b.memset('pool', B2[32:64, :], 1.0 / 32, ['B2'])
    b.memset('pool', epsb[:], SUBLN_EPS, ['epsb'])
    b.tt('dve', lj[:, 0:64], lt[:, 0:64], lt[:, 64:128], ALU.mult, ['lt'], ['lj'])
    b.tt('dve', lj[:, 64:128], lt[:, 128:192], lt[:, 192:256], ALU.mult, ['lt'], ['lj'])
    b.ts('dve', lt[:, 0:64], lj[:, 0:64], 1.0, ALU.mult, ['lj'], ['lt', 'ls0'], op1=ALU.add, accum_out=ls[:, 0:1])
    b.ts('dve', lt[:, 64:128], lj[:, 64:128], 1.0, ALU.mult, ['lj'], ['lt', 'ls1'], op1=ALU.add, accum_out=ls[:, 1:2])
    b.act(ls[:, 2:4], ls[:, 0:2], AF.Exp, ['ls0', 'ls1'], ['ls2'])
    b.tt('dve', ls[:, 4:5], ls[:, 3:4], ls[:, 2:3], ALU.subtract, ['ls2'], ['ls4'])
    b.ts('dve', ls[:, 5:6], ls[:, 4:5], -lam_init, ALU.add, ['ls4'], ['neglam'])
    b.ts('dve', sw[:, 1:2], sw[:, 0:1], 1.0 - lam_init, ALU.mult, ['sw0'], ['sw1'])
    neglam = ls[:, 5:6]
    wst = [b.sb("wst%d" % i, [128, 512], F32) for i in range(2)]
    wph = b.sb("wph", [128, 8, 512], BF16)
    hTt = [b.sb("hTt%d" % i, [128, 8, 512], BF16) for i in range(2)]
    qT = b.sb("qT", [128, SEQ], BF16)
    kT = b.sb("kT", [128, SEQ], BF16)
    zs = b.sb("zs", [128, SEQ], BF16)
    vT = b.sb("vT", [128, SEQ], BF16)
    acc = b.sb("acc", [128, SEQ], F32)
    qD = {4: b.sb("q4", [128, SEQ], BF16), 16: b.sb("q16", [128, SEQ], BF16)}
    V = b.sb("V", [128, 64, 128], BF16)
    Pb = [b.sb("Pb%d" % i, [128, 2, 512], BF16) for i in range(3)]
    stg = [b.sb("stg%d" % i, [128, 256], BF16) for i in range(8)]
    rt = [b.sb("rt%d" % i, [128, 4, 4, 8], F32) for i in range(2)]
    o1s = b.sb("o1s", [128, 512], F32)
    o2s = b.sb("o2s", [128, 512], F32)
    dens = b.sb("dens", [64, 512], F32)
    r1 = b.sb("r1", [128, 512], F32)
    ob = b.sb("ob", [128, 512], F32)
    sq = b.sb("sq", [128, 512], BF16)
    yb = [b.sb("yb%d" % i, [128, 512], BF16) for i in range(2)]
    rr = b.sb("rr", [128, 512], F32)
    ya = yb
    PS = [b.ps("PS%d" % i, [128, 1024], F32) for i in range(4)]

    def bank(i):
        return PS[i // 2][:, (i % 2) * 512:(i % 2 + 1) * 512]

    def bank_bf(i):
        return bank(i).bitcast(BF16)

    hTv = hT.rearrange("(k p) t -> p k t", p=128)
    nload = [0]

    def load_h(tt):
        s = nload[0] % 2
        nload[0] += 1
        b.dma(hTt[s][:], hTv[:, :, tt * 512:(tt + 1) * 512], [], [('hTt', s)], sem='ld_h%d' % s)
        return s

    for ph in phases:
        isB = ph < 2
        for k in range(8):
            s = k % 2
            b.dma(wst[s][:], wm[ph, k * 128:(k + 1) * 128, :], [], [('wst', s)], sem='ld_w%d' % s)
            b.cp('pool', wph[:, k, :], wst[s][:], [('wst', s)], [('wph', k)])
        wkeys = [('wph', k) for k in range(8)]
        hs = load_h(0)
        pendT = []
        for tt in range(DBG['ntt']):
            cur = hs
            if tt + 1 < DBG['ntt']:
                hs = load_h(tt + 1)
            hk = ('hTt', cur)
            for k in range(8 if 'z' in DBG['parts'] else 0):
                b.mm(bank(DBG.get('zb', 0)), wph[:, k, 384:512], hTt[cur][:, k, :], k == 0, k == 7, wkeys + [hk], [('bk', 0)])
            if 'z' in DBG['parts']:
                b.act(zs[:, tt * 512:(tt + 1) * 512], bank(DBG.get('zb', 0)), AF.Silu, [('bk', 0)], [('zs', tt)])
            for k in range(8 if 'v' in DBG['parts'] else 0):
                b.mm(bank(1), wph[:, k, 256:384], hTt[cur][:, k, :], k == 0, k == 7, wkeys + [hk], [('bk', 1)])
            if 'v' in DBG['parts']:
                b.cp('act', vT[:, tt * 512:(tt + 1) * 512], bank(1), [('bk', 1)], [('vT', tt)])
            while pendT:
                pendT.pop(0)()
            pr = tt % 2
            for s4 in range(DBG.get('ns4', 4) if 'q' in DBG['parts'] else 0):
                qk = bank(2 + s4)[:, 0:256]
                qkk = ('bk', 2 + s4)
                for k in range(8):
                    b.mm(qk, hTt[cur][:, k, s4 * 128:(s4 + 1) * 128], wph[:, k, 0:256], k == 0, k == 7,
                         wkeys + [hk], [qkk])
                t = tt * 4 + s4
                qv = qk.rearrange("p (s d) -> p s d", s=4)
                sg = stg[pr * 4 + s4]
                sgv = sg[:].rearrange("p (s d) -> p s d", s=4)
                sgk = ('stg', pr * 4 + s4)
                R = rt[s4 % 2]
                rk = ('rt', s4 % 2)
                ca_ = cosT[:, t, :]
                cb = bass.AP(tensor=ca_.tensor, offset=ca_.offset, ap=[list(ca_.ap[0]), [0, 4], [1, 8]])
                sa_ = sinT[:, t, :]
                sb_ = bass.AP(tensor=sa_.tensor, offset=sa_.offset, ap=[list(sa_.ap[0]), [0, 4], [1, 8]])
                if 'r' not in DBG['parts']:
                    b.cp(DBG.get('qcp', 'act'), sg[:], qk, [qkk], [sgk])
                    continue
                t1 = qv[:, :, 0:8]
                t2 = qv[:, :, 8:16]
                b.tt('dve', R[:, 0, :, :], t1, cb, ALU.mult, [qkk, 'cosT'], [rk])
                b.tt('dve', R[:, 1, :, :], t2, sb_, ALU.mult, [qkk, 'sinT'], [rk])
                b.tt('dve', R[:, 2, :, :], t1, sb_, ALU.mult, [qkk, 'sinT'], [rk])
                b.tt('dve', R[:, 3, :, :], t2, cb, ALU.mult, [qkk, 'cosT'], [rk])
                b.tt('dve', sgv[:, :, 0:8], R[:, 0, :, :], R[:, 1, :, :], ALU.subtract, [rk], [sgk])
                b.tt('dve', sgv[:, :, 8:16], R[:, 2, :, :], R[:, 3, :, :], ALU.add, [rk], [sgk])
                b.cp('act', sgv[:, :, 16:64], qv[:, :, 16:64], [qkk], [sgk])
            def do_tr(tt=tt, pr=pr):
                tb = 6 + pr
                tbv = bank_bf(tb)
                for s4 in range(4):
                    sl_ = pr * 4 + s4
                    b.tr(tbv[:, s4 * 128:(s4 + 1) * 128], stg[sl_][:, 0:128], idb[:], [('stg', sl_), 'idb'], [('bk', tb)])
                    b.tr(tbv[:, (4 + s4) * 128:(5 + s4) * 128], stg[sl_][:, 128:256], idb[:], [('stg', sl_), 'idb'], [('bk', tb)])
                b.cp('dve', qT[:, tt * 512:(tt + 1) * 512], tbv[:, 0:512], [('bk', tb)], [('qT', tt)])
                b.cp('dve', kT[:, tt * 512:(tt + 1) * 512], tbv[:, 512:1024], [('bk', tb)], [('kT', tt)])
            pendT.append(do_tr)
        while pendT:
            pendT.pop(0)()
        if stage < 2:
            continue
        allq = [('qT', i) for i in range(16)]
        allk = [('kT', i) for i in range(16)]
        allv = [('vT', i) for i in range(16)]
        allz = [('zs', i) for i in range(16)]
        if isB:
            for g8 in range(8):
                tb = 6 + g8 % 2
                tbv = bank_bf(tb)
                for j in range(8):
                    t = g8 * 8 + j
                    b.tr(tbv[:, j * 128:(j + 1) * 128], vT[:, t * 128:(t + 1) * 128], idb[:], allv + ['idb'], [('bk', tb)])
                b.cp('dve' if g8 % 2 else 'act', V[:, g8 * 8:(g8 + 1) * 8, :].rearrange("p a c -> p (a c)"), tbv[:, :],
                     [('bk', tb)], [('V', g8)])
            allV = [('V', i) for i in range(8)]
            if stage < 3:
                continue
            steps = []
            for qc in range(16):
                for kt in range(4 * qc + 4):
                    steps.append((qc, kt))
            LAG = 1

            def emit_S(n):
                qc, kt = steps[n]
                j = kt - 4 * qc
                c0 = 128 * j if j >= 0 else 0
                sp_ = n % 2
                for h2 in range(2):
                    Sb = bank(2 * sp_ + h2)
                    b.mm(Sb[:, c0:512], kT[64 * h2:64 * h2 + 64, kt * 128:(kt + 1) * 128],
                         qT[64 * h2:64 * h2 + 64, qc * 512 + c0:(qc + 1) * 512], True, j < 0,
                         allq + allk, [('bk', 2 * sp_ + h2)])
                if j >= 0:
                    for h2 in range(2):
                        Sb = bank(2 * sp_ + h2)
                        b.mm(Sb[:, c0:c0 + 128], idb[:], maskB[:], False, True, ['idb', 'maskB'], [('bk', 2 * sp_ + h2)])
                pb = Pb[n % 3]
                b.act(pb[:, :, c0:512], PS[sp_][:].rearrange("p (b n) -> p b n", b=2)[:, :, c0:512], AF.Exp,
                      [('bk', 2 * sp_), ('bk', 2 * sp_ + 1)], [('Pb', n % 3)], scale=0.125)

            def emit_AV(n):
                qc, kt = steps[n]
                j = kt - 4 * qc
                c0 = 128 * j if j >= 0 else 0
                first = kt == 0
                last = kt == 4 * qc + 3
                pb = Pb[n % 3]
                pk = [('Pb', n % 3)]
                b.mm(bank(4)[:, c0:512], V[:, kt, :], pb[:, 0, c0:512], first, last, allV + pk, [('bk', 4)])
                b.mm(bank(5)[:, c0:512], V[:, kt, :], pb[:, 1, c0:512], first, last, allV + pk, [('bk', 5)])
                b.mm(bank(6)[0:32, c0:512], ones[:, 0:32], pb[:, 0, c0:512], first, last, ['ones'] + pk, [('bk', 6)])
                b.mm(bank(6)[32:64, c0:512], ones[:, 0:32], pb[:, 1, c0:512], first, last, ['ones'] + pk, [('bk', 6)])
                if last:
                    epilogue(qc, n)

            pending = []

            def epilogue(qc, n):
                b.cp('dve', o1s[:], bank(4), [('bk', 4)], ['o1s'])
                b.cp('act', o2s[:], bank(5), [('bk', 5)], ['o2s'])
                b.cp('dve', dens[:], bank(6)[0:64, :], [('bk', 6)], ['dens'])

                def partB():
                    b.mm(bank(7), B1[:], dens[:], True, True, ['B1', 'dens'], [('bk', 7)])
                    b.recip(r1[:], bank(7), [('bk', 7)], ['r1'])
                    b.tt('dve', o1s[:], o1s[:], r1[:], ALU.mult, ['o1s', 'r1'], ['o1s'])

                def partC():
                    b.mm(bank(7), B2[:], dens[:], True, True, ['B2', 'dens'], [('bk', 7)])
                    b.recip(r1[:], bank(7), [('bk', 7)], ['r1'])
                    b.tt('dve', o2s[:], o2s[:], r1[:], ALU.mult, ['o2s', 'r1'], ['o2s'])
                    b.stt(ob[:], o2s[:], neglam, o1s[:], ALU.mult, ALU.add, ['o1s', 'o2s', 'neglam'], ['ob'])
                    b.tt('dve', sq[:], ob[:], ob[:], ALU.mult, ['ob'], ['sq'])

                def partD():
                    b.mm(bank(7), onesS[:], sq[:], True, True, ['onesS', 'sq'], [('bk', 7)])
                    b.act(r1[:], bank(7), AF.Ln, [('bk', 7), 'epsb'], ['r1'], bias=epsb[:, 0:1])
                    b.act(r1[:], r1[:], AF.Exp, ['r1'], ['r1'], scale=-0.5)
                    b.tt('dve', ob[:], ob[:], r1[:], ALU.mult, ['ob', 'r1'], ['ob'])
                    y = yb[qc % 2]
                    b.stt(y[:], ob[:], sw[:, 1:2], zs[:, qc * 512:(qc + 1) * 512], ALU.mult, ALU.mult,
                          ['ob', 'sw1'] + allz, [('yb', qc % 2)])
                    b.dma(yT[ph * 128:(ph + 1) * 128, qc * 512:(qc + 1) * 512], y[:], [('yb', qc % 2)], [],
                          sem='st_y%d' % (qc % 2))
                pending.append((n + 1, partB))
                pending.append((n + 2, partC))
                pending.append((n + 3, partD))

            for n in range(len(steps) + LAG):
                if n < len(steps):
                    emit_S(n)
                if n - LAG >= 0:
                    emit_AV(n - LAG)
                while pending and pending[0][0] <= n - LAG:
                    pending.pop(0)[1]()
            while pending:
                pending.pop(0)[1]()
        else:
            LAGA = 3
            PbA = [Pb[i // 2][:, i % 2, 0:256] for i in range(6)]
            for h in range(2):
                rb = 64 * h
                Vhs = [V[:, (32 * i):(32 * i + 32), :].rearrange("p a c -> p (a c)").rearrange("p (s d) -> p s d", d=64)
                       for i in range(2)]
                b.memset('pool', acc[:], 0.0, [('acc', gi, r_) for gi in range(64) for r_ in range(16)])
                blocks = []
                for di, D in enumerate((1, 4, 16)):
                    NCH = 64 // D
                    corder = (list(range(0, NCH, 2)) + list(range(1, NCH, 2))) if D < 16 else list(range(NCH))
                    for c in corder:
                        for r_ in range(D):
                            blocks.append((di, D, c, r_))

                def build_V(di, D):
                    vi = (3 * h + di) % 2
                    Vh = Vhs[vi]
                    for g16 in range(4):
                        tb = 6 + g16 % 2
                        tbv = bank_bf(tb)
                        for j in range(16):
                            slot = g16 * 16 + j
                            c, r_ = slot // D, slot % D
                            tok0 = c * 128 * D + r_
                            b.tr(tbv[:, j * 64:(j + 1) * 64], vT[rb:rb + 64, tok0:tok0 + 127 * D + 1:D], idb[rb:rb + 64, rb:rb + 64],
                                 allv + ['idb'], [('bk', tb)])
                        b.cp('act', Vh[:, g16 * 16:(g16 + 1) * 16, :].rearrange("p s d -> p (s d)"), tbv[:, :],
                             [('bk', tb)], [('Vh', vi)] + [('V', g_) for g_ in range(8)])

                def emit_SA(n):
                    di, D, c, r_ = blocks[n]
                    NCH = 64 // D
                    tok0 = c * 128 * D + r_
                    nq = 256 if c + 1 < NCH else 128
                    Sb = bank(n % 4)
                    if D == 1:
                        qsrc = qT[rb:rb + 64, tok0:tok0 + nq]
                        qkeys = allq
                    else:
                        q0 = (r_ * NCH + c) * 128
                        qsrc = qD[D][rb:rb + 64, q0:q0 + nq]
                        qkeys = [('qD', D)]
                    b.mm(Sb[:, 0:nq], kT[rb:rb + 64, tok0:tok0 + 127 * D + 1:D], qsrc,
                         True, True, qkeys + allk, [('bk', n % 4)])
                    pb = PbA[n % 6]
                    b.act(pb[:, 0:nq], Sb[:, 0:nq], AF.Exp, [('bk', n % 4)], [('PbA', n % 6)], scale=0.125)
                    b.tt('pool', pb[:, 0:nq], pb[:, 0:nq], maskA[:, 0:nq], ALU.mult, [('PbA', n % 6), 'maskA'], [('PbA', n % 6)])

                def emit_AVA(n):
                    di, D, c, r_ = blocks[n]
                    NCH = 64 // D
                    tok0 = c * 128 * D + r_
                    nq = 256 if c + 1 < NCH else 128
                    vi = (3 * h + di) % 2
                    Vh = Vhs[vi]
                    Ob = bank(4 + n % 2)
                    pb = PbA[n % 6]
                    slot = c * D + r_
                    b.mm(Ob[rb:rb + 64, 0:nq], Vh[:, slot, :], pb[:, 0:nq], True, True,
                         [('Vh', vi), ('PbA', n % 6)], [('bk', 4 + n % 2)])
                    b.mm(Ob[64 - rb:128 - rb, 0:nq], ones[:, 0:64], pb[:, 0:nq], True, True,
                         ['ones', ('PbA', n % 6)], [('bk', 4 + n % 2)])
                    g0 = tok0 // 128
                    ng = (nq * D) // 128
                    if D == 1:
                        keys = [('acc', g0 + gi, x_) for gi in range(ng) for x_ in range(16)]
                    elif D == 4:
                        keys = [('acc', g0 + gi, r_ + 4 * x_) for gi in range(ng) for x_ in range(4)]
                    else:
                        keys = [('acc', g0 + gi, r_) for gi in range(ng)]
                    av = acc[:, tok0:tok0 + (nq - 1) * D + 1:D]
                    b.tt('dve', av, Ob[:, 0:nq], av, ALU.add, [('bk', 4 + n % 2)] + keys, keys)

                def destride(D, eng):
                    NCH = 64 // D
                    src = qT[:].rearrange("p (c i r) -> p r c i", r=D, i=128)
                    dst = qD[D][:].rearrange("p (r c i) -> p r c i", r=D, i=128)
                    for r_ in range(D):
                        e_ = eng[r_ % len(eng)]
                        b.cp(e_, dst[:, r_, :, :], src[:, r_, :, :], allq, [('qD', D)])

                def normalise(cidx):
                    keys_c = [('acc', gi, x_) for gi in range(16 * cidx, 16 * cidx + 16) for x_ in range(16)]
                    for pc in range(4 * cidx, 4 * cidx + 4):
                        cs = slice(pc * 512, (pc + 1) * 512)
                        y = ya[pc % 2]
                        b.recip(rr[rb:rb + 64, :], acc[64 - rb:128 - rb, cs], keys_c, ['rr'])
                        b.tt('dve', rr[rb:rb + 64, :], acc[rb:rb + 64, cs], rr[rb:rb + 64, :], ALU.mult, keys_c + ['rr'], ['rr'])
                        b.tt('dve', y[rb:rb + 64, :], rr[rb:rb + 64, :], zs[rb:rb + 64, cs], ALU.mult, ['rr'] + allz, [('yb', pc % 2)])
                        b.dma(yT[ph * 128 + rb:ph * 128 + rb + 64, cs], y[rb:rb + 64, :], [('yb', pc % 2)], [],
                              sem='st_a%d' % (pc % 2))

                build_V(0, 1)
                nb = len(blocks)
                for n in range(nb + LAGA):
                    if n < nb:
                        emit_SA(n)
                    m = n - LAGA
                    if m >= 0:
                        emit_AVA(m)
                        if m == 4 and h == 0:
                            destride(4, ['act'])
                        if m == 30 and h == 0:
                            destride(16, ['act'])
                        if m == 8:
                            build_V(1, 4)
                        if m == 64 + 8:
                            build_V(2, 16)
                        if blocks[m][1] == 16 and blocks[m][3] == 15:
                            normalise(blocks[m][2])
    b.finish()
    return nc


def build_merge_launch(last):
    nc = bass.Bass("TRN2", target_bir_lowering=False)
    yaT = nc.dram_tensor("yaT", [DM, TOWN], BF16, kind="ExternalInput").ap()
    ybT = nc.dram_tensor("ybT", [DM, TOWN], BF16, kind="ExternalInput").ap()
    hT = nc.dram_tensor("hT", [DM, TOWN], BF16, kind="ExternalInput").ap()
    x = nc.dram_tensor("x", [TOWN, DM], F32, kind="ExternalInput").ap()
    wp = nc.dram_tensor("wp", [5, DM, DM], F32, kind="ExternalInput").ap()
    nw = nc.dram_tensor("nw", [1, DM], F32, kind="ExternalInput").ap()
    idn = nc.dram_tensor("idn", [128, 128], BF16, kind="ExternalInput").ap()
    if last:
        out = nc.dram_tensor("out", [TOWN, DM], F32, kind="ExternalOutput").ap()
    else:
        xn_d = nc.dram_tensor("xn", [TOWN, DM], F32, kind="ExternalOutput").ap()
        hTn = nc.dram_tensor("hTn", [DM, TOWN], BF16, kind="ExternalOutput").ap()
    b = B(nc)
    idb = b.sb("idb", [128, 128], BF16)
    b.dma(idb[:], idn, [], ['idb'], sem='ld_id')
    ns = NormStage(b, nw, idb, "n")
    W = [b.sb("W%d" % i, [128, 8, DM], BF16) for i in range(5)]
    wst = [b.sb("wst%d" % i, [128, DM], F32) for i in range(2)]
    n = 0
    inT = [[b.sb("in%d_%d" % (i, s), [128, 8, 512], BF16) for s in range(2)] for i in range(3)]
    srcs = [yaT.rearrange("(k p) t -> p k t", p=128), ybT.rearrange("(k p) t -> p k t", p=128),
            hT.rearrange("(k p) t -> p k t", p=128)]
    sg = [b.sb("sg%d" % i, [128, 512], F32) for i in range(4)]
    m1 = [b.sb("m1_%d" % i, [128, 512], F32) for i in range(2)]
    m2 = [b.sb("m2_%d" % i, [128, 512], F32) for i in range(2)]
    mT = b.sb("mT", [128, 8, 512], BF16)
    xt = [b.sb("xt%d" % i, [128, DM], F32) for i in range(2)]
    xo = [b.sb("xo%d" % i, [128, DM], F32) for i in range(2)]
    hb = [b.sb("hb%d" % i, [128, DM], BF16) for i in range(4)]
    hTt = [b.sb("hTt%d" % i, [128, 8, 512], BF16) for i in range(2)]
    PS = [b.ps("PS%d" % i, [128, 512], F32) for i in range(7)]
    PT = b.ps("PT", [128, 1024], BF16)
    if not last:
        hTv = hTn.rearrange("(k p) t -> p k t", p=128)
    NTQ = TOWN // 512

    def load_in(tq):
        s = tq % 2
        for i in range(3):
            b.dma(inT[i][s][:], srcs[i][:, :, tq * 512:(tq + 1) * 512], [], [('in', i, s)], sem='ld_i%d_%d' % (i, s))

    def stageA(tq):
        s = tq % 2
        if tq + 1 < NTQ:
            load_in(tq + 1)
        for oc in range(8):
            pz = (oc % 2) * 2
            ocs = slice(oc * 128, (oc + 1) * 128)
            for k in range(8):
                b.mm(PS[pz][:], W[2][:, k, ocs], inT[2][s][:, k, :], k == 0, k == 7, [('W', 2, oc), ('in', 2, s)], [('bk', pz)])
            for k in range(8):
                b.mm(PS[pz + 1][:], W[3][:, k, ocs], inT[2][s][:, k, :], k == 0, k == 7, [('W', 3, oc), ('in', 2, s)], [('bk', pz + 1)])
            b.act(sg[pz][:], PS[pz][:], AF.Sigmoid, [('bk', pz)], [('sg', pz)])
            b.act(sg[pz + 1][:], PS[pz + 1][:], AF.Sigmoid, [('bk', pz + 1)], [('sg', pz + 1)])
            pp = 4 + (oc % 2)
            for k in range(8):
                b.mm(PS[pp][:], W[0][:, k, ocs], inT[0][s][:, k, :], k == 0, k == 7, [('W', 0, oc), ('in', 0, s)], [('bk', pp)])
            b.tt('dve', m1[oc % 2][:], PS[pp][:], sg[pz][:], ALU.mult, [('bk', pp), ('sg', pz)], [('m1', oc % 2)])
            for k in range(8):
                b.mm(PS[6][:], W[1][:, k, ocs], inT[1][s][:, k, :], k == 0, k == 7, [('W', 1, oc), ('in', 1, s)], [('bk', 6)])
            b.tt('dve', m2[oc % 2][:], PS[6][:], sg[pz + 1][:], ALU.mult, [('bk', 6), ('sg', pz + 1)], [('m2', oc % 2)])
            b.tt('pool', mT[:, oc, :], m1[oc % 2][:], m2[oc % 2][:], ALU.add, [('m1', oc % 2), ('m2', oc % 2)], [('mT', oc)])

    def stageB1(tq):
        mkeys = [('mT', oc) for oc in range(8)]
        w4 = [('W', 4, k) for k in range(8)]
        for s4 in range(4):
            t = tq * 4 + s4
            xs = t % 2
            b.dma(xt[xs][:], x[t * 128:(t + 1) * 128, :], [], [('xt', xs)], sem='ld_x%d' % xs)
            for half in range(2):
                bi = (s4 % 2) * 2 + half
                pb_ = PS[bi]
                for k in range(8):
                    b.mm(pb_[:], mT[:, k, s4 * 128:(s4 + 1) * 128], W[4][:, k, half * 512:(half + 1) * 512],
                         k == 0, k == 7, mkeys + w4, [('bk', bi)])
                b.tt('dve', xo[xs][:, half * 512:(half + 1) * 512], pb_[:], xt[xs][:, half * 512:(half + 1) * 512], ALU.add,
                     [('bk', bi), ('xt', xs)], [('xo', xs, half)])
            xok = [('xo', xs, 0), ('xo', xs, 1)]
            if not last:
                b.dma(xn_d[t * 128:(t + 1) * 128, :], xo[xs][:], xok, [], sem='st_x%d' % xs)
            b.act(ns.junk[:], xo[xs][:], AF.Square, xok, ['junkn', ('nst', t % 2)], accum_out=ns.st[t % 2][:, 0:1])
            st_ = ns.st[t % 2]
            kk = ('nst', t % 2)
            b.act(st_[:, 1:2], st_[:, 0:1], AF.Ln, [kk, 'epsn'], [kk], scale=1.0 / DM, bias=ns.eps[:, 0:1])
            b.act(st_[:, 2:3], st_[:, 1:2], AF.Exp, [kk], [kk], scale=-0.5)
            if last:
                b.stt(xt[xs][:], xo[xs][:], st_[:, 2:3], ns.nw[:], ALU.mult, ALU.mult, xok + [kk, 'nwn', ('xt', xs)], [('xt', xs)])
                b.dma(out[t * 128:(t + 1) * 128, :], xt[xs][:], [('xt', xs)], [], sem='st_o%d' % xs)
            else:
                b.stt(hb[s4][:], xo[xs][:], st_[:, 2:3], ns.nw[:], ALU.mult, ALU.mult, xok + [kk, 'nwn'], [('hb', s4)])

    def stageB2(tq):
        if last:
            return
        gs = tq % 2
        for s4 in range(4):
            for k in range(8):
                b.tr(PT[:, k * 128:(k + 1) * 128], hb[s4][:, k * 128:(k + 1) * 128], idb[:], [('hb', s4), 'idb'], ['PT'])
            b.cp('act', hTt[gs][:, :, s4 * 128:(s4 + 1) * 128], PT[:].rearrange("p (k n) -> p k n", k=8), ['PT'],
                 [('hTt', gs, s4)])
        b.dma(hTv[:, :, tq * 512:(tq + 1) * 512], hTt[gs][:], [('hTt', gs, q) for q in range(4)], [],
              sem='st_h%d' % gs)

    load_in(0)
    ceng = ['act', 'dve', 'pool']
    wstv = [w_[:, 0:DM].rearrange("p (k c) -> p k c", k=8) for w_ in wst]
    for ocb in range(8):
        for wi in (2, 3, 0, 1):
            s = n % 2
            ce = ceng[n % 3]
            n += 1
            b.dma(wstv[s], wp[wi, :, ocb * 128:(ocb + 1) * 128].rearrange("(k p) c -> p k c", p=128), [], [('wst', s)],
                  sem='ld_w%d' % s)
            b.cp(ce, W[wi][:, :, ocb * 128:(ocb + 1) * 128], wstv[s], [('wst', s)], [('W', wi, ocb)])
    for k in range(8):
        s = n % 2
        ce = ceng[n % 3]
        n += 1
        b.dma(wst[s][:], wp[4, k * 128:(k + 1) * 128, :], [], [('wst', s)], sem='ld_w%d' % s)
        b.cp(ce, W[4][:, k, :], wst[s][:], [('wst', s)], [('W', 4, k)])
    stageA(0)
    for tq in range(NTQ):
        stageB1(tq)
        if tq + 1 < NTQ:
            stageA(tq + 1)
        stageB2(tq)
    b.finish()
    return nc


def _consts():
    idn = np.eye(128, dtype=np.float32).astype(ml_dtypes.bfloat16)
    ki = np.arange(128)[:, None]
    qi = np.arange(128)[None, :]
    mka = np.concatenate([np.where(ki <= qi, 1.0, 0.0), np.where(ki >= qi, 1.0, 0.0)], axis=1).astype(np.float32)
    mkb = np.where(ki <= qi, 0.0, NEG).astype(np.float32)
    inv = (1.0 / (np.float32(500000.0) ** (np.arange(0, 16, 2, dtype=np.float32) / np.float32(16)))).astype(np.float32)
    pos = np.arange(SEQ, dtype=np.float32)
    ang = (pos[:, None] * inv[None, :]).astype(np.float32)
    cos = np.cos(ang).astype(np.float32).reshape(64, 128, 8).transpose(1, 0, 2).reshape(128, 512)
    sin = np.sin(ang).astype(np.float32).reshape(64, 128, 8).transpose(1, 0, 2).reshape(128, 512)
    return dict(idn=idn, mka=mka.astype(ml_dtypes.bfloat16), mkb=mkb.astype(ml_dtypes.bfloat16),
                cosd=np.ascontiguousarray(cos), sind=np.ascontiguousarray(sin))


def _mixer_weights(w_in_l, g):
    cols = []
    for j in range(2):
        hb = 2 * g + j
        for nm in ('qB', 'kB', 'vB', 'zB'):
            cols.append(np.arange(OFF[nm] + hb * 128, OFF[nm] + hb * 128 + 128))
    for j in range(2):
        ha = 4 * g + 2 * j
        for nm in ('qA', 'kA', 'vA', 'zA'):
            cols.append(np.arange(OFF[nm] + ha * 64, OFF[nm] + ha * 64 + 128))
    cols = np.concatenate(cols)
    w = w_in_l[:, cols]
    return np.ascontiguousarray(w.reshape(DM, 4, 512).transpose(1, 0, 2))


_NC_CACHE = {}


def _get(name, fn, *a):
    key = (name,) + a
    if key not in _NC_CACHE:
        _NC_CACHE[key] = fn(*a)
    return _NC_CACHE[key]


def kernel(x, norm_w, w_in, lambda_q1, lambda_k1, lambda_q2, lambda_k2, subln_w,
           w_proj_a, w_proj_b, w_out, final_norm_w):
    x = np.asarray(x, dtype=np.float32)
    cst = _consts()
    cores = list(range(8))
    xs = [np.ascontiguousarray(x[c // 4, (c % 4) * TOWN:(c % 4 + 1) * TOWN, :]) for c in cores]
    nca = build_norm_launch()
    res = run_bass_kernel_spmd(nca, [dict(x=xs[c], nw=np.ascontiguousarray(norm_w[0][None, :]), idn=cst['idn'])
                                     for c in cores], core_ids=cores)
    hown = [res.results[c]['hT'] for c in cores]
    for l in range(DEPTH):
        hfull = [np.ascontiguousarray(np.concatenate(hown[4 * bb:4 * bb + 4], axis=1)) for bb in range(2)]
        lamv = np.ascontiguousarray(np.concatenate([lambda_q1[l], lambda_k1[l], lambda_q2[l], lambda_k2[l]])[None, :].astype(np.float32))
        subw = np.ascontiguousarray(np.asarray(subln_w[l], dtype=np.float32).reshape(128, 1))
        ncm = build_mixer_launch(l)
        in_maps = []
        for c in cores:
            in_maps.append(dict(hT=hfull[c // 4], wm=_mixer_weights(np.asarray(w_in[l]), c % 4), lamv=lamv, subw=subw,
                                idn=cst['idn'], mka=cst['mka'], mkb=cst['mkb'], cosd=cst['cosd'], sind=cst['sind']))
        res = run_bass_kernel_spmd(ncm, in_maps, core_ids=cores)
        yTs = [res.results[c]['yT'] for c in cores]
        last = l == DEPTH - 1
        ncp = build_merge_launch(last)
        wp = np.ascontiguousarray(np.stack([np.asarray(w_proj_a[l]), np.asarray(w_proj_b[l]),
                                            np.asarray(w_in[l][:, OFF['ga']:OFF['ga'] + DM]),
                                            np.asarray(w_in[l][:, OFF['gb']:OFF['gb'] + DM]),
                                            np.asarray(w_out[l])]).astype(np.float32))
        nwn = np.ascontiguousarray((final_norm_w if last else norm_w[l + 1])[None, :].astype(np.float32))
        in_maps = []
        for c in cores:
            bb, g = c // 4, c % 4
            ts = slice(g * TOWN, (g + 1) * TOWN)
            yaT = np.ascontiguousarray(np.concatenate([yTs[4 * bb + gg][256:512, ts] for gg in range(4)], axis=0))
            ybT = np.ascontiguousarray(np.concatenate([yTs[4 * bb + gg][0:256, ts] for gg in range(4)], axis=0))
            in_maps.append(dict(yaT=yaT, ybT=ybT, hT=hown[c], x=xs[c], wp=wp, nw=nwn, idn=cst['idn']))
        res = run_bass_kernel_spmd(ncp, in_maps, core_ids=cores)
        if last:
            outs = [res.results[c]['out'] for c in cores]
        else:
            xs = [res.results[c]['xn'] for c in cores]
            hown = [res.results[c]['hTn'] for c in cores]
    out = np.empty((2, SEQ, DM), dtype=np.float32)
    for c in cores:
        out[c // 4, (c % 4) * TOWN:(c % 4 + 1) * TOWN, :] = outs[c]
    return out
```
